# Optimizing a Trainium2 kernel written in Bass

```python
import jax, jax.numpy as jnp
from jax import lax
import numpy as np

D_MODEL = 1024
BATCH = 16
SEQ = 2048
DEPTH = 1

D_FF = 2816
PLE_DIM = 256
HG_HEADS = 4
HG_KDIM = 128
HG_VDIM = 128
HG_FDIM = HG_HEADS * HG_KDIM
HG_WIDTH = HG_HEADS * HG_VDIM
CHUNK = 64
MLA_HEADS = 4
Q_LORA = 256
KV_LORA = 128
NOPE_DIM = 128
ROPE_DIM = 64
V_DIM = 128
QK_DIM = NOPE_DIM + ROPE_DIM
MLA_WIDTH = MLA_HEADS * V_DIM
QBLOCK = 128
ROPE_THETA = 10000.0
MIX_WIDTH = HG_WIDTH + MLA_WIDTH
IN_SIZES = (HG_FDIM, HG_FDIM, HG_WIDTH, HG_WIDTH, Q_LORA, KV_LORA, ROPE_DIM)
IN_WIDTH = sum(IN_SIZES)
EPS = 1e-6

kernel_name = "hymba_hgrn2_mla_macaron_ple"


def rms_norm(x, g):
    xf = x.astype(jnp.float32)
    y = xf * lax.rsqrt(jnp.mean(xf * xf, axis=-1, keepdims=True) + EPS)
    return (y * g.astype(jnp.float32)).astype(x.dtype)


def swiglu(x, w_gate, w_up, w_down):
    return (jax.nn.silu(x @ w_gate) * (x @ w_up)) @ w_down


def apply_rope(x, cos, sin):
    x1, x2 = jnp.split(x.astype(jnp.float32), 2, axis=-1)
    return jnp.concatenate([x1 * cos - x2 * sin, x2 * cos + x1 * sin], axis=-1).astype(x.dtype)


def hgrn2_chunkwise(q, log_f, k, v):
    B, S, H, DK = q.shape
    DV = v.shape[-1]
    n_chunks = S // CHUNK

    def to_chunks(t):
        return t.reshape(B, n_chunks, CHUNK, H, t.shape[-1]).transpose(1, 0, 3, 2, 4)

    causal = jnp.tril(jnp.ones((CHUNK, CHUNK), dtype=bool))[:, :, None]

    def step(state, inp):
        q_c, g_c, k_c, v_c = inp
        b = jnp.cumsum(g_c, axis=2)
        diff = b[:, :, :, None, :] - b[:, :, None, :, :]
        decay = jnp.exp(jnp.where(causal, diff, -jnp.inf))
        scores = jnp.einsum('bhtd,bhsd,bhtsd->bhts', q_c, k_c, decay)
        o = (jnp.einsum('bhts,bhsv->bhtv', scores, v_c)
             + jnp.einsum('bhtd,bhdv->bhtv', q_c * jnp.exp(b), state))
        b_last = b[:, :, -1:, :]
        state = (state * jnp.exp(b_last)[:, :, 0, :, None]
                 + jnp.einsum('bhsd,bhsv->bhdv', k_c * jnp.exp(b_last - b), v_c))
        return state, o

    s0 = jnp.zeros((B, H, DK, DV), jnp.float32)
    _, o = lax.scan(step, s0, tuple(map(to_chunks, (q, log_f, k, v))))
    return o.transpose(1, 0, 3, 2, 4).reshape(B, S, H, DV)


def causal_block_attention(q, k, v):
    S = q.shape[2]
    scale = QK_DIM ** -0.5
    outs = []
    for j in range(S // QBLOCK):
        lo, hi = j * QBLOCK, (j + 1) * QBLOCK
        s = jnp.einsum('bhqd,bhkd->bhqk', q[:, :, lo:hi], k[:, :, :hi]).astype(jnp.float32) * scale
        mask = (lo + jnp.arange(QBLOCK))[:, None] >= jnp.arange(hi)[None, :]
        s = jnp.where(mask, s, -jnp.inf)
        pr = jax.nn.softmax(s, axis=-1).astype(v.dtype)
        outs.append(jnp.einsum('bhqk,bhkv->bhqv', pr, v[:, :, :hi]))
    return jnp.concatenate(outs, axis=2)


def setup_inputs(seed: int = 0) -> dict:
    key = jax.random.key(seed)
    ks = jax.random.split(key, 24)

    def w(k, shape, fan_in):
        return jax.random.normal(k, shape, jnp.float32) * fan_in ** -0.5

    def gain(k, shape):
        return 1.0 + 0.02 * jax.random.normal(k, shape, jnp.float32)

    offsets = jax.random.randint(ks[2], (BATCH, 1), 0, 1024, dtype=jnp.int32)
    positions = offsets + jnp.arange(SEQ, dtype=jnp.int32)[None, :]
    return {
        "x": jax.random.normal(ks[0], (BATCH, SEQ, D_MODEL), jnp.float32),
        "p": jax.random.normal(ks[1], (DEPTH, BATCH, SEQ, PLE_DIM), jnp.float32),
        "positions": positions,
        "ln_ffn1": gain(ks[3], (DEPTH, D_MODEL)),
        "w1_gate": w(ks[4], (DEPTH, D_MODEL, D_FF), D_MODEL),
        "w1_up": w(ks[5], (DEPTH, D_MODEL, D_FF), D_MODEL),
        "w1_down": w(ks[6], (DEPTH, D_FF, D_MODEL), D_FF),
        "ln_mix": gain(ks[7], (DEPTH, D_MODEL)),
        "w_in": w(ks[8], (DEPTH, D_MODEL, IN_WIDTH), D_MODEL),
        "hg_lb_logits": 0.5 * jax.random.normal(ks[9], (DEPTH + 1, HG_FDIM), jnp.float32),
        "hg_out_norm": gain(ks[10], (DEPTH, HG_HEADS, HG_VDIM)),
        "q_a_norm": gain(ks[11], (DEPTH, Q_LORA)),
        "w_q_up": w(ks[12], (DEPTH, Q_LORA, MLA_HEADS * QK_DIM), Q_LORA),
        "kv_a_norm": gain(ks[13], (DEPTH, KV_LORA)),
        "w_kv_up": w(ks[14], (DEPTH, KV_LORA, MLA_HEADS * (NOPE_DIM + V_DIM)), KV_LORA),
        "w_out": w(ks[15], (DEPTH, MIX_WIDTH, D_MODEL), MIX_WIDTH),
        "ln_ffn2": gain(ks[16], (DEPTH, D_MODEL)),
        "w2_gate": w(ks[17], (DEPTH, D_MODEL, D_FF), D_MODEL),
        "w2_up": w(ks[18], (DEPTH, D_MODEL, D_FF), D_MODEL),
        "w2_down": w(ks[19], (DEPTH, D_FF, D_MODEL), D_FF),
        "ln_ple": gain(ks[20], (DEPTH, D_MODEL)),
        "w_ple_gate": w(ks[21], (DEPTH, D_MODEL, D_MODEL), D_MODEL),
        "w_ple_proj": w(ks[22], (DEPTH, PLE_DIM, D_MODEL), PLE_DIM),
        "ln_final": gain(ks[23], (D_MODEL,)),
    }


def reference(x, p, positions, ln_ffn1, w1_gate, w1_up, w1_down, ln_mix, w_in,
              hg_lb_logits, hg_out_norm, q_a_norm, w_q_up, kv_a_norm, w_kv_up, w_out,
              ln_ffn2, w2_gate, w2_up, w2_down, ln_ple, w_ple_gate, w_ple_proj, ln_final):
    B, S, _ = x.shape
    split_at = [int(v) for v in np.cumsum(IN_SIZES)[:-1]]

    lower_bounds = jnp.cumsum(jax.nn.softmax(hg_lb_logits.astype(jnp.float32), axis=0), axis=0)

    half = ROPE_DIM // 2
    inv_freq = ROPE_THETA ** (-jnp.arange(half, dtype=jnp.float32) / half)
    ang = positions.astype(jnp.float32)[..., None] * inv_freq
    cos, sin = jnp.cos(ang), jnp.sin(ang)

    h = x
    for i in range(DEPTH):
        h = h + 0.5 * swiglu(rms_norm(h, ln_ffn1[i]), w1_gate[i], w1_up[i], w1_down[i])

        u = rms_norm(h, ln_mix[i]) @ w_in[i]
        hq, hf, hi_, hg, cq, ckv, kr = jnp.split(u, split_at, axis=-1)

        lb = lower_bounds[i]
        f_raw = hf.astype(jnp.float32)
        log_f = jnp.log(lb + (1.0 - lb) * jax.nn.sigmoid(f_raw))
        k_in = (1.0 - lb) * jax.nn.sigmoid(-f_raw)
        o_hg = hgrn2_chunkwise(
            hq.astype(jnp.float32).reshape(B, S, HG_HEADS, HG_KDIM),
            log_f.reshape(B, S, HG_HEADS, HG_KDIM),
            k_in.reshape(B, S, HG_HEADS, HG_KDIM),
            hi_.astype(jnp.float32).reshape(B, S, HG_HEADS, HG_VDIM)).astype(h.dtype)
        o_hg = rms_norm(o_hg, hg_out_norm[i]).reshape(B, S, HG_WIDTH) * jax.nn.silu(hg)

        q = (rms_norm(cq, q_a_norm[i]) @ w_q_up[i]).reshape(B, S, MLA_HEADS, QK_DIM)
        q_nope, q_rope = jnp.split(q, [NOPE_DIM], axis=-1)
        q_rope = apply_rope(q_rope, cos[:, :, None, :], sin[:, :, None, :])
        kv = (rms_norm(ckv, kv_a_norm[i]) @ w_kv_up[i]).reshape(B, S, MLA_HEADS, NOPE_DIM + V_DIM)
        k_nope, v = jnp.split(kv, [NOPE_DIM], axis=-1)
        k_rope = apply_rope(kr, cos, sin)
        k_rope = jnp.broadcast_to(k_rope[:, :, None, :], (B, S, MLA_HEADS, ROPE_DIM))
        qf = jnp.concatenate([q_nope, q_rope], axis=-1).transpose(0, 2, 1, 3)
        kf = jnp.concatenate([k_nope, k_rope], axis=-1).transpose(0, 2, 1, 3)
        o_mla = causal_block_attention(qf, kf, v.transpose(0, 2, 1, 3))
        o_mla = o_mla.transpose(0, 2, 1, 3).reshape(B, S, MLA_WIDTH)

        h = h + jnp.concatenate([o_hg, o_mla], axis=-1) @ w_out[i]

        h = h + 0.5 * swiglu(rms_norm(h, ln_ffn2[i]), w2_gate[i], w2_up[i], w2_down[i])

        gate = jax.nn.sigmoid(rms_norm(h, ln_ple[i]) @ w_ple_gate[i])
        h = h + gate * (p[i].astype(h.dtype) @ w_ple_proj[i])

    return rms_norm(h, ln_final)
```

```python
import contextlib
import os
import numpy as np
import concourse.bass as bass
import concourse.mybir as mybir
from concourse.bass_utils import run_bass_kernel_spmd

F32 = mybir.dt.float32
BF16 = mybir.dt.bfloat16
I32 = mybir.dt.int32
AF = mybir.ActivationFunctionType
ALU = mybir.AluOpType

D = 1024
S = 2048
NT = 4
DFF = 2816
NHC = 22
EPS = 1e-6
SCALE = 192 ** -0.5
PI = float(np.pi)
C1 = 6.28125
C2 = float(2 * np.pi - 6.28125)
HG_LEVEL = int(os.environ.get('HG_LEVEL', '2'))
MAXOPS = int(os.environ.get('MK_MAXOPS', '100000000'))

G_FFN1, G_MIX, G_FFN2, G_PLE, G_FIN, G_QA, G_KVA, G_HG = 0, 8, 16, 24, 32, 40, 42, 43
K_ID, K_BD, K_CM, K_ON, K_CI, K_MISC = 0, 128, 256, 384, 512, 514
M_EPS, M_ONE, M_INVF, M_SIGN, M_ZERO = 0, 1, 2, 3, 4


class Prog:
    ENGS = ('pe', 'act', 'dve', 'pool', 'sp')

    def __init__(self, nc):
        self.nc = nc
        self.ops = {e: [] for e in self.ENGS}
        self.cnt = {e: 0 for e in self.ENGS}
        self.lastw = {}
        self.readers = {}
        self.dma_cnt = {}
        self.seen = {e: {} for e in self.ENGS}
        self.semnames = set('E:' + e for e in self.ENGS)
        self.out_tokens = []
        self.alias = {}

    def _x(self, keys):
        out = []
        for k in keys:
            out.extend(self.alias.get(k, (k,)))
        return out

    def _deps(self, eng, reads, writes):
        toks = []
        for k in reads:
            t = self.lastw.get(k)
            if t is not None:
                toks.append(t)
        for k in writes:
            t = self.lastw.get(k)
            if t is not None:
                toks.append(t)
            toks.extend(self.readers.get(k, ()))
        need = {}
        for (sem, val, teng) in toks:
            if teng == 'pe' and eng == 'pe':
                continue
            if need.get(sem, 0) < val:
                need[sem] = val
        waits = []
        for sem, val in need.items():
            if self.seen[eng].get(sem, 0) >= val:
                continue
            self.seen[eng][sem] = val
            waits.append((sem, val))
        return waits

    def _commit(self, tok, reads, writes):
        for k in reads:
            self.readers.setdefault(k, []).append(tok)
        for k in writes:
            self.lastw[k] = tok
            self.readers[k] = []

    def op(self, eng, fn, reads=(), writes=()):
        self.total = getattr(self, 'total', 0) + 1
        if self.total > MAXOPS:
            return None
        reads, writes = self._x(reads), self._x(writes)
        waits = self._deps(eng, reads, writes)
        self.cnt[eng] += 1
        tok = ('E:' + eng, self.cnt[eng], eng)
        self.ops[eng].append(('c', waits, fn, tok))
        self._commit(tok, reads, writes)
        return tok

    def dma(self, eng, fn, n, semkey, reads=(), writes=(), out=False):
        self.total = getattr(self, 'total', 0) + 1
        if self.total > MAXOPS:
            return None
        reads, writes = self._x(reads), self._x(writes)
        waits = self._deps(eng, reads, writes)
        sem = 'D:' + semkey
        self.semnames.add(sem)
        self.dma_cnt[sem] = self.dma_cnt.get(sem, 0) + n
        tok = (sem, 16 * self.dma_cnt[sem], 'dma')
        self.ops[eng].append(('d', waits, fn, tok))
        self._commit(tok, reads, writes)
        if out:
            self.out_tokens.append(tok)
        return tok

    def barrier(self):
        allt = {}
        for e in self.ENGS:
            if self.cnt[e]:
                allt['E:' + e] = self.cnt[e]
        for sem, c in self.dma_cnt.items():
            allt[sem] = 16 * c
        for e in self.ENGS:
            waits = []
            for sem, val in allt.items():
                if sem == 'E:' + e:
                    continue
                if self.seen[e].get(sem, 0) >= val:
                    continue
                self.seen[e][sem] = val
                waits.append((sem, val))
            if waits:
                self.ops[e].append(('w', waits, None, None))

    def emit(self):
        nc = self.nc
        fin = {}
        for (sem, val, _) in self.out_tokens:
            fin[sem] = max(fin.get(sem, 0), val)
        with contextlib.ExitStack() as es:
            sems = {}
            for name in sorted(self.semnames):
                sems[name] = es.enter_context(nc.semaphore(name.replace(':', '_')))
            block = es.enter_context(nc.Block())

            def run(engname, eng):
                for (kind, waits, fn, tok) in self.ops[engname]:
                    for (sem, val) in waits:
                        eng.wait_ge(sems[sem], val)
                    if kind == 'c':
                        fn(eng).then_inc(sems[tok[0]], 1)
                    elif kind == 'd':
                        s = sems[tok[0]]
                        fn(eng, lambda inst, s=s: inst.then_inc(s, 16))
                if engname == 'sp':
                    for sem, val in fin.items():
                        eng.wait_ge(sems[sem], val)

            @block.tensor
            def _(e):
                run('pe', e)

            @block.scalar
            def _(e):
                run('act', e)

            @block.vector
            def _(e):
                run('dve', e)

            @block.gpsimd
            def _(e):
                run('pool', e)

            @block.sync
            def _(e):
                run('sp', e)


class Arena:
    def __init__(self, nc, base=16512, limit=229344):
        self.nc, self.off, self.limit, self.n = nc, base, limit, 0

    def alloc(self, name, shape, dtype):
        esz = 2 if dtype == BF16 else 4
        size = esz
        for s in shape[1:]:
            size *= s
        size = (size + 63) // 64 * 64
        assert self.off + size <= self.limit, (name, self.off, size, self.limit)
        self.n += 1
        t = self.nc.alloc_sbuf_tensor_at(f"{name}_{self.n}", list(shape), dtype, offset=self.off)
        self.off += size
        return t

    def mark(self):
        return self.off

    def reset(self, m):
        self.off = m


def build_program(stop_after=None, nseq=2):
    nc = bass.Bass("TRN2", target_bir_lowering=False)
    dram = lambda name, shape, dt=F32, kind="ExternalInput": nc.dram_tensor(name, list(shape), dt, kind=kind).ap()
    x_d = dram("x", [2, S, D])
    p_d = dram("p", [2, S, 256])
    pos_d = dram("pos", [2, S], I32)
    w1g_d, w1u_d, w1d_d = dram("w1g", [D, DFF]), dram("w1u", [D, DFF]), dram("w1d", [DFF, D])
    w2g_d, w2u_d, w2d_d = dram("w2g", [D, DFF]), dram("w2u", [D, DFF]), dram("w2d", [DFF, D])
    wina_d = dram("wina", [D, 2048])
    winb_d = dram("winb", [D, 512])
    wq_d = dram("wq", [256, 1024])
    wkvk_d = dram("wkvk", [128, 512])
    wkvv_d = dram("wkvv", [128, 512])
    wot_d = dram("wot", [512, D])
    wob_d = dram("wob", [512, D])
    wpg_d = dram("wpg", [D, D])
    wpp_d = dram("wpp", [256, D])
    gains_d = dram("gains", [128, 47])
    lbl_d = dram("lbl", [2, 512])
    cst_d = dram("cst", [128, 530])
    y_d = dram("y", [2, S, D], F32, "ExternalOutput")

    P = Prog(nc)
    A = Arena(nc)
    es = contextlib.ExitStack()
    ps = [es.enter_context(nc.psum_tensor(f"ps{i}", [128, 512], F32)) for i in range(8)]
    PK = [f'ps{i}' for i in range(8)]
    for i in range(8):
        P.alias[f'ps{i}'] = tuple(f'ps{i}_{q}' for q in range(4))

    hT = A.alloc("hT", [128, 8, S], F32)
    cst = A.alloc("cst", [128, 530], F32)
    cstb = A.alloc("cstb", [128, 512], BF16)
    gains = A.alloc("gains", [128, 47], F32)
    LBt = A.alloc("LBt", [128, 512], F32)
    LBm1 = A.alloc("LBm1", [128, 512], F32)
    ident_f = cst[:, K_ID:K_ID + 128]
    bdmask = cst[:, K_BD:K_BD + 128]
    cmaskT = cst[:, K_CM:K_CM + 128]
    chunkind = cst[:, K_CI:K_CI + 2]
    misc = lambda j, n=128: cst[0:n, K_MISC + j:K_MISC + j + 1]
    ident_b = cstb[:, 0:128]
    ones_b = cstb[:, 384:512]
    gcol = lambda j: gains[:, j:j + 1]
    phase_base = A.mark()
    lbl = A.alloc("lbl", [128, 2, 512], F32)

    P.dma('sp', lambda e, inc: (inc(e.dma_start(out=cst[:], in_=cst_d[:, :])),
                                inc(e.dma_start(out=gains[:], in_=gains_d[:, :])),
                                inc(e.dma_start(out=lbl[:].rearrange("p a b -> p (a b)"),
                                                in_=lbl_d.rearrange("a b -> (a b)").partition_broadcast(128)))),
          3, 'setup', writes=['cst', 'gains', 'lbl'])
    P.dma('pool', lambda e, inc: inc(e.dma_start(out=cstb[:], in_=cst_d[:, 0:512])), 1, 'setupb', writes=['cstb'])
    P.op('dve', lambda e: e.tensor_tensor(out=LBt[:], in0=lbl[:, 1, :], in1=lbl[:, 0, :], op=ALU.subtract), reads=['lbl'], writes=['LBt'])
    P.op('act', lambda e: e.activation(out=LBt[:], in_=LBt[:], func=AF.Exp), reads=['LBt'], writes=['LBt'])
    P.op('dve', lambda e: e.tensor_scalar(out=LBt[:], in0=LBt[:], scalar1=1.0, scalar2=None, op0=ALU.add), reads=['LBt'], writes=['LBt'])
    P.op('dve', lambda e: e.reciprocal(out=LBt[:], in_=LBt[:]), reads=['LBt'], writes=['LBt'])
    P.op('dve', lambda e: e.tensor_scalar(out=LBm1[:], in0=LBt[:], scalar1=-1.0, scalar2=None, op0=ALU.add), reads=['LBt'], writes=['LBm1'])

    P.barrier()

    hk = lambda c, t: f'h{c}_{t}'
    xk = lambda c, t: f'xn{c}_{t}'
    tsl = lambda t: slice(t * 512, (t + 1) * 512)

    def rms_norm_tile(t, gbase, xn, sq, lnv, psn, out_f32=False):
        for c in range(8):
            P.op('act', lambda e, c=c: e.activation(out=sq[:, c, :], in_=hT[:, c, tsl(t)], func=AF.Square),
                 reads=[hk(c, t)], writes=[f'sq{c}'])
        for c in range(8):
            P.op('pe', lambda e, c=c: e.matmul(ps[psn][:], lhsT=ones_b, rhs=sq[:, c, :], start=(c == 0), stop=(c == 7)),
                 reads=[f'sq{c}', 'cstb'], writes=[PK[psn]])
        P.op('act', lambda e: e.activation(out=lnv[:], in_=ps[psn][:], func=AF.Ln, scale=1.0 / D, bias=misc(M_EPS)),
             reads=[PK[psn], 'cst'], writes=['lnv'])
        P.op('act', lambda e: e.activation(out=lnv[:], in_=lnv[:], func=AF.Exp, scale=-0.5), reads=['lnv'], writes=['lnv'])
        for c in range(8):
            dst = xn[:, c, :] if out_f32 else xn[:, c, tsl(t)]
            P.op('dve', lambda e, c=c, dst=dst: e.scalar_tensor_tensor(out=dst, in0=hT[:, c, tsl(t)], scalar=gcol(gbase + c),
                                                                        in1=lnv[:], op0=ALU.mult, op1=ALU.mult),
                 reads=[hk(c, t), 'lnv', 'gains'], writes=[f'yn{c}' if out_f32 else xk(c, t)])

    def load_w_rows(dst, src, nk, ncols, semkey, wkey, col0=0, eng='pool'):
        pieces = []
        for k in range(nk):
            for c0 in range(0, ncols, 1024):
                cn = min(1024, ncols - c0)
                pieces.append((k, c0, cn))

        def fn(e, inc):
            for (k, c0, cn) in pieces:
                inc(e.dma_start(out=dst[:, k, c0:c0 + cn], in_=src[k * 128:(k + 1) * 128, col0 + c0:col0 + c0 + cn]))
        P.dma(eng, fn, len(pieces), semkey, writes=[wkey])

    def run_seq(sq_i):
        A.reset(phase_base)
        xtok = [A.alloc("xtok", [128, D], F32) for _ in range(4)]
        for blk in range(16):
            xt = xtok[blk % 4]
            t = blk // 4
            P.dma('sp', lambda e, inc, xt=xt, blk=blk: inc(e.dma_start(out=xt[:], in_=x_d[sq_i, blk * 128:(blk + 1) * 128, :])),
                  1, f'xtok{blk % 4}', writes=[f'xtok{blk % 4}'])
            for half in range(2):
                bank = (blk * 2 + half) % 4
                for cc in range(4):
                    c = half * 4 + cc
                    P.op('pe', lambda e, xt=xt, c=c, cc=cc, bank=bank: e.transpose(out=ps[bank][:, cc * 128:(cc + 1) * 128],
                                                                                   in_=xt[:, c * 128:(c + 1) * 128], identity=ident_f),
                         reads=[f'xtok{blk % 4}', 'cst'], writes=[PK[bank]])
                dst = hT[:, half * 4:half * 4 + 4, blk * 128:(blk + 1) * 128]
                src = ps[bank][:].rearrange("p (a b) -> p a b", a=4)
                wk = [hk(half * 4 + cc, t) for cc in range(4)]
                if half == 0:
                    P.op('act', lambda e, dst=dst, src=src: e.activation(out=dst, in_=src, func=AF.Copy), reads=[PK[bank]], writes=wk)
                else:
                    P.op('dve', lambda e, dst=dst, src=src: e.tensor_copy(out=dst, in_=src), reads=[PK[bank]], writes=wk)
        P.barrier()

        def ffn(gbase, wg_d, wu_d, wd_d):
            A.reset(phase_base)
            xn = A.alloc("xn", [128, 8, S], BF16)
            hmid = A.alloc("hmid", [128, 11, S], BF16)
            wgu = [A.alloc("wgu", [128, 2, 8, 256], BF16) for _ in range(2)]
            wd = A.alloc("wd", [128, 11, D], BF16)
            sq = A.alloc("sq", [128, 8, 512], BF16)
            lnv = A.alloc("lnv", [128, 512], F32)
            sg = [A.alloc("sg", [128, 512], BF16) for _ in range(2)]
            for t in range(NT):
                rms_norm_tile(t, gbase, xn, sq, lnv, 6)
            groups = []
            for hh in range(2):
                for (j0, n) in [(0, 2), (2, 2), (4, 2), (6, 2), (8, 2), (10, 1)]:
                    groups.append((hh, j0, n))

            def load_group(gi):
                hh, j0, n = groups[gi]
                slot = gi % 2
                col0 = (hh * 11 + j0) * 128

                def fn(e, inc):
                    for k in range(8):
                        inc(e.dma_start(out=wgu[slot][:, 0, k, 0:n * 128], in_=wg_d[k * 128:(k + 1) * 128, col0:col0 + n * 128]))
                        inc(e.dma_start(out=wgu[slot][:, 1, k, 0:n * 128], in_=wu_d[k * 128:(k + 1) * 128, col0:col0 + n * 128]))
                P.dma('pool', fn, 16, f'wgu{slot}', writes=[f'wgu{slot}'])

            def load_wd(hh):
                for j in range(11):
                    r0 = (hh * 11 + j) * 128
                    P.dma('pool', lambda e, inc, j=j, r0=r0: inc(e.dma_start(out=wd[:, j, :], in_=wd_d[r0:r0 + 128, :])),
                          1, f'wd{j}', writes=[f'wd{j}'])

            cnt = 0
            cnt2 = 0
            load_group(0)
            for gi, (hh, j0, n) in enumerate(groups):
                if gi + 1 < len(groups):
                    load_group(gi + 1)
                if j0 == 0:
                    load_wd(hh)
                slot = gi % 2
                for jj in range(n):
                    j = j0 + jj
                    for t in range(NT):
                        bg, bu = cnt % 2, 2 + cnt % 2
                        for k in range(8):
                            P.op('pe', lambda e, k=k, jj=jj, t=t, bg=bg, slot=slot: e.matmul(
                                ps[bg][:], lhsT=wgu[slot][:, 0, k, jj * 128:(jj + 1) * 128], rhs=xn[:, k, tsl(t)], start=(k == 0), stop=(k == 7)),
                                reads=[f'wgu{slot}', xk(k, t)], writes=[PK[bg]])
                        for k in range(8):
                            P.op('pe', lambda e, k=k, jj=jj, t=t, bu=bu, slot=slot: e.matmul(
                                ps[bu][:], lhsT=wgu[slot][:, 1, k, jj * 128:(jj + 1) * 128], rhs=xn[:, k, tsl(t)], start=(k == 0), stop=(k == 7)),
                                reads=[f'wgu{slot}', xk(k, t)], writes=[PK[bu]])
                        sgt = sg[cnt % 2]
                        P.op('act', lambda e, bg=bg, sgt=sgt: e.activation(out=sgt[:], in_=ps[bg][:], func=AF.Silu),
                             reads=[PK[bg]], writes=[f'sg{cnt % 2}'])
                        P.op('dve', lambda e, bu=bu, sgt=sgt, j=j, t=t: e.tensor_tensor(out=hmid[:, j, tsl(t)], in0=sgt[:], in1=ps[bu][:], op=ALU.mult),
                             reads=[PK[bu], f'sg{cnt % 2}'], writes=[f'hm{j}_{t}'])
                        cnt += 1
                if j0 == 10:
                    for t in range(NT):
                        for o in range(8):
                            bd = 4 + cnt2 % 2
                            for j in range(11):
                                P.op('pe', lambda e, j=j, o=o, t=t, bd=bd: e.matmul(
                                    ps[bd][:], lhsT=wd[:, j, o * 128:(o + 1) * 128], rhs=hmid[:, j, tsl(t)], start=(j == 0), stop=(j == 10)),
                                    reads=[f'wd{j}', f'hm{j}_{t}'], writes=[PK[bd]])
                            P.op('dve', lambda e, o=o, t=t, bd=bd: e.scalar_tensor_tensor(
                                out=hT[:, o, tsl(t)], in0=ps[bd][:], scalar=0.5, in1=hT[:, o, tsl(t)], op0=ALU.mult, op1=ALU.add),
                                reads=[PK[bd], hk(o, t)], writes=[hk(o, t)])
                            cnt2 += 1
            P.barrier()

        if stop_after != 'x':
            ffn(G_FFN1, w1g_d, w1u_d, w1d_d)

        if stop_after not in ('x', 'ffn1'):
            A.reset(phase_base)
            xn = A.alloc("xnm", [128, 8, S], BF16)
            mix_base = A.mark()
            sq = A.alloc("sq", [128, 8, 512], BF16)
            lnv = A.alloc("lnv", [128, 512], F32)
            for t in range(NT):
                rms_norm_tile(t, G_MIX, xn, sq, lnv, 6)
            P.barrier()

            A.reset(mix_base)
            Wa = A.alloc("Wa", [128, 8, 2048], BF16)
            Wot = A.alloc("Wot", [128, 4, D], BF16)
            tm = {n: A.alloc(n, [128, 512], F32) for n in ('e', 't1', 'l1', 'l2', 'sg', 'kk', 'ebn', 'ebp')}
            logf = A.alloc("logf", [128, 512], F32)
            Qp_tok = A.alloc("Qp_tok", [128, 512], BF16)
            Kp_tok = A.alloc("Kp_tok", [128, 4, 512], BF16)
            V_tok = A.alloc("V_tok", [128, 4, 512], BF16)
            QpT = A.alloc("QpT", [128, 4, 512], BF16)
            KpT = A.alloc("KpT", [128, 4, 512], BF16)
            sgT = A.alloc("sgT", [128, 4, 512], BF16)
            AT = [A.alloc("AT", [128, 128], BF16) for _ in range(4)]
            S32 = A.alloc("S32", [128, 4, 128], F32)
            S32e = A.alloc("S32e", [128, 4, 128], F32)
            Sbf = A.alloc("Sbf", [128, 4, 128], BF16)
            Ecol = A.alloc("Ecol", [128, 32], F32)
            sqo = A.alloc("sqo", [128, 512], BF16)
            lnvo = A.alloc("lnvo", [128, 512], F32)
            to = A.alloc("to", [128, 512], F32)
            ohg = A.alloc("ohg", [128, 4, 512], BF16)
            load_w_rows(Wa, wina_d, 8, 2048, 'Wa', 'Wa')
            load_w_rows(Wot, wot_d, 4, D, 'Wot', 'Wot')
            P.op('dve', lambda e: e.memset(S32[:], 0.0), writes=[f'S32_{h}' for h in range(4)])
            P.op('pool', lambda e: e.memset(Sbf[:], 0.0), writes=[f'Sbf_{h}' for h in range(4)])
            psTb = ps[4][:].bitcast(BF16)
            psTk = ps[5][:].bitcast(BF16)

            def tile_a(t):
                e_, t1, l1, l2, sg_, kk, ebn, ebp = (tm[n] for n in ('e', 't1', 'l1', 'l2', 'sg', 'kk', 'ebn', 'ebp'))
                QB = [1, 7]

                def proj(blk):
                    r = slice(t * 512 + blk * 128, t * 512 + (blk + 1) * 128)
                    for (bank, col0) in ((0, 512), (2, 1024), (QB[blk % 2], 0)):
                        for k in range(8):
                            P.op('pe', lambda e, k=k, bank=bank, col0=col0: e.matmul(
                                ps[bank][:], lhsT=xn[:, k, r], rhs=Wa[:, k, col0:col0 + 512], start=(k == 0), stop=(k == 7)),
                                reads=[xk(k, t), 'Wa'], writes=[PK[bank]])

                def chain1(blk):
                    P.op('act', lambda e: e.activation(out=e_[:], in_=ps[0][:], func=AF.Exp, scale=-1.0), reads=[PK[0]], writes=['e'])
                    P.op('act', lambda e: e.activation(out=V_tok[:, blk, :], in_=ps[2][:], func=AF.Copy), reads=[PK[2]], writes=[f'V_tok{blk}'])
                    P.op('dve', lambda e: e.tensor_tensor(out=t1[:], in0=e_[:], in1=LBt[:], op=ALU.mult), reads=['e', 'LBt'], writes=['t1'])
                    P.op('act', lambda e: e.activation(out=l2[:], in_=e_[:], func=AF.Ln, bias=misc(M_ONE)), reads=['e', 'cst'], writes=['l2'])
                    P.op('act', lambda e: e.activation(out=l1[:], in_=t1[:], func=AF.Ln, bias=misc(M_ONE)), reads=['t1', 'cst'], writes=['l1'])
                    P.op('dve', lambda e: e.tensor_tensor(out=logf[:], in0=l1[:], in1=l2[:], op=ALU.subtract), reads=['l1', 'l2'], writes=['logf'])
                    P.op('act', lambda e: e.activation(out=sg_[:], in_=l2[:], func=AF.Exp, scale=-1.0), reads=['l2'], writes=['sg'])
                    P.op('dve', lambda e: e.scalar_tensor_tensor(out=kk[:], in0=sg_[:], scalar=-1.0, in1=LBm1[:], op0=ALU.add, op1=ALU.mult),
                         reads=['sg', 'LBm1'], writes=['kk'])

                def cumsum(blk):
                    P.op('pe', lambda e: e.matmul(ps[3][:], lhsT=bdmask, rhs=logf[:], start=True, stop=True), reads=['cst', 'logf'], writes=[PK[3]])

                def chain2(blk):
                    qb = QB[blk % 2]
                    P.op('act', lambda e: e.activation(out=ebn[:], in_=ps[3][:], func=AF.Exp, scale=-1.0), reads=[PK[3]], writes=['ebn'])
                    P.op('act', lambda e: e.activation(out=ebp[:], in_=ps[3][:], func=AF.Exp), reads=[PK[3]], writes=['ebp'])
                    P.op('dve', lambda e: e.tensor_tensor(out=Kp_tok[:, blk, :], in0=kk[:], in1=ebn[:], op=ALU.mult),
                         reads=['kk', 'ebn'], writes=[f'Kp_tok{blk}'])
                    P.op('dve', lambda e: e.tensor_tensor(out=Qp_tok[:], in0=ps[qb][:], in1=ebp[:], op=ALU.mult), reads=[PK[qb], 'ebp'], writes=['Qp_tok'])
                    for h in range(4):
                        P.op('pe', lambda e, h=h: e.transpose(out=psTk[:, h * 128:(h + 1) * 128],
                                                              in_=Kp_tok[:, blk, h * 128:(h + 1) * 128], identity=ident_b),
                             reads=[f'Kp_tok{blk}', 'cstb'], writes=[PK[5]])
                    for h in range(4):
                        P.op('pe', lambda e, h=h: e.transpose(out=psTb[:, h * 128:(h + 1) * 128], in_=Qp_tok[:, h * 128:(h + 1) * 128], identity=ident_b),
                             reads=['Qp_tok', 'cstb'], writes=[PK[4]])
                    bs = slice(blk * 128, (blk + 1) * 128)
                    P.op('act', lambda e: e.activation(out=KpT[:, :, bs], in_=psTk[:, 0:512].rearrange("p (a b) -> p a b", a=4), func=AF.Copy),
                         reads=[PK[5]], writes=[f'KpT{blk}'])
                    P.op('dve', lambda e: e.tensor_copy(out=QpT[:, :, bs], in_=psTb[:, 0:512].rearrange("p (a b) -> p a b", a=4)),
                         reads=[PK[4]], writes=[f'QpT{blk}'])
                    for h in range(4):
                        P.op('pe', lambda e, h=h: e.matmul(ps[6][:, h * 8 + blk * 2:h * 8 + blk * 2 + 2], lhsT=logf[:, h * 128:(h + 1) * 128],
                                                           rhs=chunkind, start=True, stop=True),
                             reads=['logf', 'cst'], writes=[PK[6]])

                proj(0)
                for blk in range(4):
                    chain1(blk)
                    if blk + 1 < 4:
                        proj(blk + 1)
                    cumsum(blk)
                    chain2(blk)
                P.op('act', lambda e: e.activation(out=Ecol[:], in_=ps[6][:, 0:32], func=AF.Exp), reads=[PK[6]], writes=['Ecol'])
                for h in range(4):
                    for k in range(8):
                        P.op('pe', lambda e, k=k, h=h: e.matmul(ps[5][:], lhsT=Wa[:, k, 1536 + h * 128:1536 + (h + 1) * 128], rhs=xn[:, k, tsl(t)],
                                                                start=(k == 0), stop=(k == 7)),
                             reads=[xk(k, t), 'Wa'], writes=[PK[5]])
                    P.op('act', lambda e, h=h: e.activation(out=sgT[:, h, :], in_=ps[5][:], func=AF.Silu), reads=[PK[5]], writes=[f'sgT{h}'])
                if HG_LEVEL < 1:
                    return
                HS = [slice(h * 128, (h + 1) * 128) for h in range(4)]
                KVB = [7, 4, 5, 7]

                def state_update(blk, c):
                    for h in range(4):
                        ec = Ecol[:, h * 8 + blk * 2 + c:h * 8 + blk * 2 + c + 1]
                        P.op('act', lambda e, h=h, ec=ec: e.activation(out=S32e[:, h, :], in_=S32[:, h, :], func=AF.Identity, scale=ec),
                             reads=[f'S32_{h}', 'Ecol'], writes=[f'S32e_{h}'])
                        P.op('dve', lambda e, h=h, ec=ec: e.scalar_tensor_tensor(out=S32[:, h, :], in0=ps[KVB[h]][:, HS[h]], scalar=ec, in1=S32e[:, h, :],
                                                                               op0=ALU.mult, op1=ALU.add),
                             reads=[PK[KVB[h]], 'Ecol', f'S32e_{h}'], writes=[f'S32_{h}'])
                        P.op('pool', lambda e, h=h: e.tensor_copy(out=Sbf[:, h, :], in_=S32[:, h, :]), reads=[f'S32_{h}'], writes=[f'Sbf_{h}'])

                def kv_mm(blk, c, h):
                    pr = slice(c * 64, (c + 1) * 64)
                    P.op('pe', lambda e, h=h, pr=pr, blk=blk: e.matmul(ps[KVB[h]][:, HS[h]], lhsT=Kp_tok[pr, blk, HS[h]], rhs=V_tok[pr, blk, HS[h]],
                                                                       start=True, stop=True),
                         reads=[f'Kp_tok{blk}', f'V_tok{blk}'], writes=[PK[KVB[h]]])

                for blk in range(4):
                    bs = slice(blk * 128, (blk + 1) * 128)
                    for h in range(4):
                        P.op('pe', lambda e, h=h, bs=bs: e.matmul(ps[6][:, HS[h]], lhsT=KpT[:, h, bs], rhs=QpT[:, h, bs], start=True, stop=True),
                             reads=[f'KpT{blk}', f'QpT{blk}'], writes=[PK[6]])
                    for h in range(4):
                        P.op('dve', lambda e, h=h: e.tensor_tensor(out=AT[h][:], in0=ps[6][:, HS[h]], in1=bdmask, op=ALU.mult),
                             reads=[PK[6], 'cst'], writes=[f'AT{h}'])
                    for h in range(4):
                        cs = slice(blk * 128, blk * 128 + 64)
                        P.op('pe', lambda e, h=h, bs=bs, blk=blk: e.matmul(ps[h][:, bs], lhsT=V_tok[:, blk, HS[h]], rhs=AT[h][:], start=True, stop=False),
                             reads=[f'V_tok{blk}', f'AT{h}'], writes=[PK[h]])
                        P.op('pe', lambda e, h=h, cs=cs: e.matmul(ps[h][:, cs], lhsT=Sbf[:, h, :], rhs=QpT[:, h, cs], start=False, stop=False),
                             reads=[f'Sbf_{h}', f'QpT{blk}'], writes=[PK[h]])
                        kv_mm(blk, 0, h)
                    state_update(blk, 0)
                    for h in range(4):
                        cs = slice(blk * 128 + 64, blk * 128 + 128)
                        P.op('pe', lambda e, h=h, cs=cs: e.matmul(ps[h][:, cs], lhsT=Sbf[:, h, :], rhs=QpT[:, h, cs], start=False, stop=True),
                             reads=[f'Sbf_{h}', f'QpT{blk}'], writes=[PK[h]])
                        kv_mm(blk, 1, h)
                    state_update(blk, 1)
                if HG_LEVEL < 2:
                    return
                lnv4 = [(tm[n], n) for n in ('e', 't1', 'l1', 'l2')]
                sq4 = [(tm[n][:].bitcast(BF16)[:, 0:512], n) for n in ('sg', 'kk', 'ebn', 'ebp')]
                for h in range(4):
                    sqa, sqk = sq4[h]
                    P.op('act', lambda e, h=h, sqa=sqa: e.activation(out=sqa, in_=ps[h][:], func=AF.Square), reads=[PK[h]], writes=[sqk])
                for h in range(4):
                    sqa, sqk = sq4[h]
                    P.op('pe', lambda e, h=h, sqa=sqa: e.matmul(ps[4 + h][:], lhsT=ones_b, rhs=sqa, start=True, stop=True), reads=[sqk, 'cstb'], writes=[PK[4 + h]])
                for h in range(4):
                    lv, lk = lnv4[h]
                    P.op('act', lambda e, h=h, lv=lv: e.activation(out=lv[:], in_=ps[4 + h][:], func=AF.Ln, scale=1.0 / 128, bias=misc(M_EPS)),
                         reads=[PK[4 + h], 'cst'], writes=[lk])
                for h in range(4):
                    lv, lk = lnv4[h]
                    P.op('act', lambda e, lv=lv: e.activation(out=lv[:], in_=lv[:], func=AF.Exp, scale=-0.5), reads=[lk], writes=[lk])
                for h in range(4):
                    lv, lk = lnv4[h]
                    P.op('dve', lambda e, h=h, lv=lv: e.scalar_tensor_tensor(out=lv[:], in0=ps[h][:], scalar=gcol(G_HG + h), in1=lv[:], op0=ALU.mult, op1=ALU.mult),
                         reads=[PK[h], 'gains', lk], writes=[lk])
                for h in range(4):
                    lv, lk = lnv4[h]
                    P.op('dve', lambda e, h=h, lv=lv: e.tensor_tensor(out=ohg[:, h, :], in0=lv[:], in1=sgT[:, h, :], op=ALU.mult),
                         reads=[lk, f'sgT{h}'], writes=[f'ohg{h}'])
                for o in range(8):
                    bank = 5 if o % 2 == 0 else 7
                    for h in range(4):
                        P.op('pe', lambda e, o=o, h=h, bank=bank: e.matmul(ps[bank][:], lhsT=Wot[:, h, o * 128:(o + 1) * 128], rhs=ohg[:, h, :],
                                                                           start=(h == 0), stop=(h == 3)),
                             reads=['Wot', f'ohg{h}'], writes=[PK[bank]])
                    P.op('dve', lambda e, o=o, bank=bank: e.tensor_tensor(out=hT[:, o, tsl(t)], in0=ps[bank][:], in1=hT[:, o, tsl(t)], op=ALU.add),
                         reads=[PK[bank], hk(o, t)], writes=[hk(o, t)])
            for t in range(NT):
                tile_a(t)
            P.barrier()

            if stop_after != 'hgrn':
                A.reset(mix_base)
                Wb = A.alloc("Wb", [128, 8, 512], BF16)
                Wq = A.alloc("Wq", [128, 2, 1024], BF16)
                Wkk = A.alloc("Wkk", [128, 1, 512], BF16)
                Wkv = A.alloc("Wkv", [128, 1, 512], BF16)
                Wob = A.alloc("Wob", [128, 4, D], BF16)
                knT = A.alloc("knT", [128, 4, S], BF16)
                krT = A.alloc("krT", [128, S], BF16)
                Vc = A.alloc("Vc", [128, 16, 512], BF16)
                cq = A.alloc("cq", [128, 3, 512], F32)
                sqb = A.alloc("sqb", [128, 3, 512], BF16)
                lnq = A.alloc("lnq", [128, 512], F32)
                cqn = A.alloc("cqn", [128, 3, 512], BF16)
                qnT = A.alloc("qnT", [128, 4, 512], BF16)
                qrT = A.alloc("qrT", [128, 4, 512], BF16)
                posi = A.alloc("posi", [64, 512], I32)
                ang = A.alloc("ang", [64, 512], F32)
                ni = posi
                nf = A.alloc("nf", [64, 512], F32)
                rr = A.alloc("rr", [64, 512], F32)
                mm_ = A.alloc("mm", [64, 512], F32)
                Ct = A.alloc("Ct", [64, 512], F32)
                St = A.alloc("St", [64, 512], F32)
                ra, rb = rr, mm_
                PT = [A.alloc("PT", [128, 512], BF16) for _ in range(3)]
                rden = lnq
                omla = A.alloc("omla", [128, 4, 512], BF16)
                load_w_rows(Wb, winb_d, 8, 512, 'Wb', 'Wb')
                load_w_rows(Wq, wq_d, 2, 1024, 'Wq', 'Wq')
                load_w_rows(Wkk, wkvk_d, 1, 512, 'Wkk', 'Wkk')
                load_w_rows(Wkv, wkvv_d, 1, 512, 'Wkv', 'Wkv')
                load_w_rows(Wob, wob_d, 4, D, 'Wob', 'Wob')
                P.op('pool', lambda e: e.memset(krT[64:65, :], 1.0), writes=['krT_one'])
                P.op('pool', lambda e: e.memset(qrT[64:65, :, :], 0.0), writes=['qrT_one'])

                def range_reduce(dst, shift, key):
                    P.op('dve', lambda e: e.tensor_scalar(out=rr[:], in0=ang[:], scalar1=shift, scalar2=None, op0=ALU.add), reads=['ang'], writes=['rr'])
                    P.op('dve', lambda e: e.tensor_scalar(out=ni[:], in0=rr[:], scalar1=float(1.0 / (2 * np.pi)), scalar2=None, op0=ALU.mult),
                         reads=['rr'], writes=['posi'])
                    P.op('dve', lambda e: e.tensor_copy(out=nf[:], in_=ni[:]), reads=['posi'], writes=['nf'])
                    P.op('dve', lambda e: e.scalar_tensor_tensor(out=rr[:], in0=nf[:], scalar=-C1, in1=rr[:], op0=ALU.mult, op1=ALU.add),
                         reads=['nf', 'rr'], writes=['rr'])
                    P.op('dve', lambda e: e.scalar_tensor_tensor(out=rr[:], in0=nf[:], scalar=-C2, in1=rr[:], op0=ALU.mult, op1=ALU.add),
                         reads=['nf', 'rr'], writes=['rr'])
                    P.op('dve', lambda e: e.tensor_scalar(out=mm_[:], in0=rr[:], scalar1=PI, scalar2=-2 * PI, op0=ALU.is_gt, op1=ALU.mult),
                         reads=['rr'], writes=['mm'])
                    P.op('dve', lambda e: e.tensor_tensor(out=nf[:], in0=rr[:], in1=mm_[:], op=ALU.add), reads=['rr', 'mm'], writes=['nf'])
                    P.op('dve', lambda e: e.tensor_scalar(out=mm_[:], in0=rr[:], scalar1=-PI, scalar2=2 * PI, op0=ALU.is_lt, op1=ALU.mult),
                         reads=['rr'], writes=['mm'])
                    P.op('dve', lambda e: e.tensor_tensor(out=nf[:], in0=nf[:], in1=mm_[:], op=ALU.add), reads=['nf', 'mm'], writes=['nf'])
                    P.op('dve', lambda e: e.tensor_scalar(out=nf[:], in0=nf[:], scalar1=PI, scalar2=-PI, op0=ALU.min, op1=ALU.max), reads=['nf'], writes=['nf'])
                    P.op('act', lambda e: e.activation(out=dst[:], in_=nf[:], func=AF.Sin), reads=['nf'], writes=[key])

                def rope(dst, psa, psb, ka, kb, wkey):
                    P.op('dve', lambda e: e.tensor_tensor(out=ra[:], in0=psa, in1=Ct[:], op=ALU.mult), reads=[ka, 'Ct'], writes=['rr'])
                    P.op('dve', lambda e: e.tensor_tensor(out=rb[:], in0=psb, in1=St[:], op=ALU.mult), reads=[kb, 'St'], writes=['mm'])
                    P.op('dve', lambda e: e.tensor_tensor(out=dst, in0=ra[:], in1=rb[:], op=ALU.add), reads=['rr', 'mm'], writes=[wkey])

                pcnt_box = [0]

                def tile_b(t):
                    P.dma('sp', lambda e, inc, t=t: inc(e.dma_start(out=posi[:], in_=pos_d[sq_i, tsl(t)].partition_broadcast(64))), 1, 'posi', writes=['posi'])
                    P.op('dve', lambda e: e.tensor_copy(out=ang[:], in_=posi[:]), reads=['posi'], writes=['ang'])
                    P.op('dve', lambda e: e.tensor_scalar(out=ang[:], in0=ang[:], scalar1=misc(M_INVF, 64), scalar2=None, op0=ALU.mult),
                         reads=['ang', 'cst'], writes=['ang'])
                    range_reduce(Ct, PI / 2, 'Ct')
                    range_reduce(St, 0.0, 'St')
                    P.op('dve', lambda e: e.tensor_scalar(out=St[:], in0=St[:], scalar1=misc(M_SIGN, 64), scalar2=None, op0=ALU.mult),
                         reads=['St', 'cst'], writes=['St'])
                    for oc in range(3):
                        bank = oc % 2
                        for k in range(8):
                            P.op('pe', lambda e, k=k, oc=oc, bank=bank: e.matmul(ps[bank][:], lhsT=Wb[:, k, oc * 128:(oc + 1) * 128], rhs=xn[:, k, tsl(t)],
                                                                                 start=(k == 0), stop=(k == 7)),
                                 reads=['Wb', xk(k, t)], writes=[PK[bank]])
                        P.op('act', lambda e, oc=oc, bank=bank: e.activation(out=cq[:, oc, :], in_=ps[bank][:], func=AF.Copy), reads=[PK[bank]], writes=[f'cq{oc}'])
                        P.op('act', lambda e, oc=oc, bank=bank: e.activation(out=sqb[:, oc, :], in_=ps[bank][:], func=AF.Square), reads=[PK[bank]], writes=[f'sqb{oc}'])
                    for oc in range(2):
                        P.op('pe', lambda e, oc=oc: e.matmul(ps[2][:], lhsT=ones_b, rhs=sqb[:, oc, :], start=(oc == 0), stop=(oc == 1)),
                             reads=[f'sqb{oc}', 'cstb'], writes=[PK[2]])
                    P.op('act', lambda e: e.activation(out=lnq[:], in_=ps[2][:], func=AF.Ln, scale=1.0 / 256, bias=misc(M_EPS)), reads=[PK[2], 'cst'], writes=['lnq'])
                    P.op('act', lambda e: e.activation(out=lnq[:], in_=lnq[:], func=AF.Exp, scale=-0.5), reads=['lnq'], writes=['lnq'])
                    for oc in range(2):
                        P.op('dve', lambda e, oc=oc: e.scalar_tensor_tensor(out=cqn[:, oc, :], in0=cq[:, oc, :], scalar=gcol(G_QA + oc), in1=lnq[:],
                                                                            op0=ALU.mult, op1=ALU.mult),
                             reads=[f'cq{oc}', 'gains', 'lnq'], writes=[f'cqn{oc}'])
                    P.op('pe', lambda e: e.matmul(ps[2][:], lhsT=ones_b, rhs=sqb[:, 2, :], start=True, stop=True), reads=['sqb2', 'cstb'], writes=[PK[2]])
                    P.op('act', lambda e: e.activation(out=lnq[:], in_=ps[2][:], func=AF.Ln, scale=1.0 / 128, bias=misc(M_EPS)), reads=[PK[2], 'cst'], writes=['lnq'])
                    P.op('act', lambda e: e.activation(out=lnq[:], in_=lnq[:], func=AF.Exp, scale=-0.5), reads=['lnq'], writes=['lnq'])
                    P.op('dve', lambda e: e.scalar_tensor_tensor(out=cqn[:, 2, :], in0=cq[:, 2, :], scalar=gcol(G_KVA), in1=lnq[:], op0=ALU.mult, op1=ALU.mult),
                         reads=['cq2', 'gains', 'lnq'], writes=['cqn2'])
                    for (bank, c0) in ((0, 384), (1, 448)):
                        for k in range(8):
                            P.op('pe', lambda e, k=k, bank=bank, c0=c0: e.matmul(ps[bank][0:64, :], lhsT=Wb[:, k, c0:c0 + 64], rhs=xn[:, k, tsl(t)],
                                                                                 start=(k == 0), stop=(k == 7)),
                                 reads=['Wb', xk(k, t)], writes=[PK[bank]])
                    rope(krT[0:64, tsl(t)], ps[0][0:64, :], ps[1][0:64, :], PK[0], PK[1], f'krT{t}')
                    for h in range(4):
                        for k2 in range(2):
                            P.op('pe', lambda e, k2=k2, h=h: e.matmul(ps[2][:], lhsT=Wq[:, k2, h * 192:h * 192 + 128], rhs=cqn[:, k2, :], start=(k2 == 0), stop=(k2 == 1)),
                                 reads=['Wq', f'cqn{k2}'], writes=[PK[2]])
                        P.op('act', lambda e, h=h: e.activation(out=qnT[:, h, :], in_=ps[2][:], func=AF.Copy), reads=[PK[2]], writes=[f'qnT{h}'])
                        for (bank, c0) in ((0, h * 192 + 128), (1, 768 + h * 64)):
                            for k2 in range(2):
                                P.op('pe', lambda e, k2=k2, bank=bank, c0=c0: e.matmul(ps[bank][0:64, :], lhsT=Wq[:, k2, c0:c0 + 64], rhs=cqn[:, k2, :],
                                                                                       start=(k2 == 0), stop=(k2 == 1)),
                                     reads=['Wq', f'cqn{k2}'], writes=[PK[bank]])
                        rope(qrT[0:64, h, :], ps[0][0:64, :], ps[1][0:64, :], PK[0], PK[1], f'qrT{h}')
                        P.op('pe', lambda e, h=h: e.matmul(ps[2][:], lhsT=Wkk[:, 0, h * 128:(h + 1) * 128], rhs=cqn[:, 2, :], start=True, stop=True),
                             reads=['Wkk', 'cqn2'], writes=[PK[2]])
                        P.op('act', lambda e, h=h: e.activation(out=knT[:, h, tsl(t)], in_=ps[2][:], func=AF.Copy), reads=[PK[2]], writes=[f'knT{h}_{t}'])
                    for blk in range(4):
                        bank = blk % 2
                        P.op('pe', lambda e, blk=blk, bank=bank: e.matmul(ps[bank][:], lhsT=cqn[:, 2, blk * 128:(blk + 1) * 128], rhs=Wkv[:, 0, :], start=True, stop=True),
                             reads=['Wkv', 'cqn2'], writes=[PK[bank]])
                        P.op('dve', lambda e, blk=blk, bank=bank: e.tensor_copy(out=Vc[:, t * 4 + blk, :], in_=ps[bank][:]), reads=[PK[bank]], writes=[f'Vc{t * 4 + blk}'])
                    nkb = 4 * t + 4
                    steps = [(h, kb) for h in range(4) for kb in range(nkb)]
                    LA = 2

                    def qk_step(idx):
                        h, kb = steps[idx]
                        i = kb - 4 * t
                        q0 = 0 if i < 0 else i * 128
                        qs = slice(q0, 512)
                        ks = slice(kb * 128, (kb + 1) * 128)
                        pcnt = pcnt_box[0] + idx
                        pb = 3 + pcnt % 3
                        pt = PT[pcnt % 3]
                        ptk = f'PT{pcnt % 3}'
                        tk = kb // 4
                        P.op('pe', lambda e: e.matmul(ps[pb][:, qs], lhsT=knT[:, h, ks], rhs=qnT[:, h, qs], start=True, stop=False),
                             reads=[f'knT{h}_{tk}', f'qnT{h}'], writes=[PK[pb]])
                        P.op('pe', lambda e: e.matmul(ps[pb][:, qs], lhsT=krT[0:65, ks], rhs=qrT[0:65, h, qs], start=False, stop=True),
                             reads=[f'krT{tk}', 'krT_one', f'qrT{h}', 'qrT_one'], writes=[PK[pb]])
                        P.op('act', lambda e: e.activation(out=pt[:, qs], in_=ps[pb][:, qs], func=AF.Exp, scale=SCALE),
                             reads=[PK[pb]], writes=[ptk])
                        if i >= 0:
                            ds_ = slice(q0, q0 + 128)
                            P.op('pool', lambda e: e.tensor_tensor(out=pt[:, ds_], in0=pt[:, ds_], in1=cmaskT, op=ALU.mult),
                                 reads=[ptk, 'cst'], writes=[ptk])

                    def pv_step(idx):
                        h, kb = steps[idx]
                        i = kb - 4 * t
                        q0 = 0 if i < 0 else i * 128
                        qs = slice(q0, 512)
                        pcnt = pcnt_box[0] + idx
                        pt = PT[pcnt % 3]
                        ptk = f'PT{pcnt % 3}'
                        bo, bd_ = (6, 7) if h % 2 == 0 else (0, 1)
                        P.op('pe', lambda e: e.matmul(ps[bo][:, qs], lhsT=Vc[:, kb, h * 128:(h + 1) * 128], rhs=pt[:, qs],
                                                      start=(kb == 0), stop=(kb == nkb - 1)),
                             reads=[f'Vc{kb}', ptk], writes=[PK[bo]])
                        P.op('pe', lambda e: e.matmul(ps[bd_][:, qs], lhsT=ones_b, rhs=pt[:, qs], start=(kb == 0), stop=(kb == nkb - 1)),
                             reads=['cstb', ptk], writes=[PK[bd_]])
                        if kb == nkb - 1:
                            P.op('dve', lambda e: e.reciprocal(out=rden[:], in_=ps[bd_][:]), reads=[PK[bd_]], writes=['lnq'])
                            P.op('dve', lambda e: e.tensor_tensor(out=omla[:, h, :], in0=ps[bo][:], in1=rden[:], op=ALU.mult),
                                 reads=[PK[bo], 'lnq'], writes=[f'omla{h}'])

                    for idx in range(len(steps) + LA):
                        if idx < len(steps):
                            qk_step(idx)
                        if idx - LA >= 0:
                            pv_step(idx - LA)
                    pcnt_box[0] += len(steps)
                    for o in range(8):
                        bank = o % 2
                        for h in range(4):
                            P.op('pe', lambda e, o=o, h=h, bank=bank: e.matmul(ps[bank][:], lhsT=Wob[:, h, o * 128:(o + 1) * 128], rhs=omla[:, h, :],
                                                                               start=(h == 0), stop=(h == 3)),
                                 reads=['Wob', f'omla{h}'], writes=[PK[bank]])
                        P.op('dve', lambda e, o=o, bank=bank: e.tensor_tensor(out=hT[:, o, tsl(t)], in0=ps[bank][:], in1=hT[:, o, tsl(t)], op=ALU.add),
                             reads=[PK[bank], hk(o, t)], writes=[hk(o, t)])
                for t in range(NT):
                    tile_b(t)
                P.barrier()

        if stop_after is None or stop_after in ('ffn2', 'ple'):
            ffn(G_FFN2, w2g_d, w2u_d, w2d_d)

        A.reset(phase_base)
        xnp = A.alloc("xnp", [128, 8, S], BF16)
        sq = A.alloc("sq", [128, 8, 512], BF16)
        lnv = A.alloc("lnv", [128, 512], F32)
        Wg = A.alloc("Wg", [128, 8, D], BF16)
        Wp = A.alloc("Wp", [128, 2, D], BF16)
        ptok = [A.alloc("ptok", [128, 256], F32) for _ in range(2)]
        pT = A.alloc("pT", [128, 2, 512], BF16)
        sig = A.alloc("sig", [128, 512], F32)
        tpl = A.alloc("tpl", [128, 512], F32)
        yn = A.alloc("yn", [128, 8, 512], F32)
        ytok = [A.alloc("ytok", [128, D], F32) for _ in range(2)]
        do_ple = stop_after is None or stop_after == 'ple'
        if do_ple:
            load_w_rows(Wg, wpg_d, 8, D, 'Wg', 'Wg')
            load_w_rows(Wp, wpp_d, 2, D, 'Wp', 'Wp')
        ycnt_box = [0]

        def ple_part(t):
            if do_ple:
                rms_norm_tile(t, G_PLE, xnp, sq, lnv, 6)
                for blk in range(4):
                    pk = ptok[blk % 2]
                    r0 = t * 512 + blk * 128
                    P.dma('sp', lambda e, inc, pk=pk, r0=r0: inc(e.dma_start(out=pk[:], in_=p_d[sq_i, r0:r0 + 128, :])), 1, f'ptok{blk % 2}', writes=[f'ptok{blk % 2}'])
                    for c2 in range(2):
                        P.op('pe', lambda e, pk=pk, c2=c2: e.transpose(out=ps[7][:, c2 * 128:(c2 + 1) * 128], in_=pk[:, c2 * 128:(c2 + 1) * 128], identity=ident_f),
                             reads=[f'ptok{blk % 2}', 'cst'], writes=[PK[7]])
                    P.op('act', lambda e, blk=blk: e.activation(out=pT[:, :, blk * 128:(blk + 1) * 128], in_=ps[7][:, 0:256].rearrange("p (a b) -> p a b", a=2), func=AF.Copy),
                         reads=[PK[7]], writes=['pT'])
                for o in range(8):
                    bg, bp = o % 2, 2 + o % 2
                    for k in range(8):
                        P.op('pe', lambda e, k=k, o=o, bg=bg: e.matmul(ps[bg][:], lhsT=Wg[:, k, o * 128:(o + 1) * 128], rhs=xnp[:, k, tsl(t)], start=(k == 0), stop=(k == 7)),
                             reads=['Wg', xk(k, t)], writes=[PK[bg]])
                    for k2 in range(2):
                        P.op('pe', lambda e, k2=k2, o=o, bp=bp: e.matmul(ps[bp][:], lhsT=Wp[:, k2, o * 128:(o + 1) * 128], rhs=pT[:, k2, :], start=(k2 == 0), stop=(k2 == 1)),
                             reads=['Wp', 'pT'], writes=[PK[bp]])
                    P.op('act', lambda e, bg=bg: e.activation(out=sig[:], in_=ps[bg][:], func=AF.Sigmoid), reads=[PK[bg]], writes=['sig'])
                    P.op('dve', lambda e, bp=bp: e.tensor_tensor(out=tpl[:], in0=sig[:], in1=ps[bp][:], op=ALU.mult), reads=['sig', PK[bp]], writes=['tpl'])
                    P.op('dve', lambda e, o=o: e.tensor_tensor(out=hT[:, o, tsl(t)], in0=tpl[:], in1=hT[:, o, tsl(t)], op=ALU.add),
                         reads=['tpl', hk(o, t)], writes=[hk(o, t)])

        def fin_part(t):
            if stop_after is None:
                rms_norm_tile(t, G_FIN, yn, sq, lnv, 6, out_f32=True)
                src_of = lambda c, bs: yn[:, c, bs]
                rk = lambda c: [f'yn{c}']
            else:
                src_of = lambda c, bs: hT[:, c, slice(t * 512 + bs.start, t * 512 + bs.stop)]
                rk = lambda c: [hk(c, t)]
            for blk in range(4):
                bs = slice(blk * 128, (blk + 1) * 128)
                yt = ytok[ycnt_box[0] % 2]
                ytk = f'ytok{ycnt_box[0] % 2}'
                ycnt_box[0] += 1
                for half in range(2):
                    bank = 4 + half
                    for cc in range(4):
                        c = half * 4 + cc
                        sap = src_of(c, bs)
                        P.op('pe', lambda e, cc=cc, bank=bank, sap=sap: e.transpose(out=ps[bank][:, cc * 128:(cc + 1) * 128], in_=sap, identity=ident_f),
                             reads=rk(c) + ['cst'], writes=[PK[bank]])
                    if half == 0:
                        P.op('act', lambda e, yt=yt, bank=bank: e.activation(out=yt[:, 0:512], in_=ps[bank][:], func=AF.Copy), reads=[PK[bank]], writes=[ytk + 'a'])
                    else:
                        P.op('dve', lambda e, yt=yt, bank=bank: e.tensor_copy(out=yt[:, 512:1024], in_=ps[bank][:]), reads=[PK[bank]], writes=[ytk + 'b'])
                r0 = t * 512 + blk * 128
                P.dma('sp', lambda e, inc, yt=yt, r0=r0: inc(e.dma_start(out=y_d[sq_i, r0:r0 + 128, :], in_=yt[:])), 1, 'yout' + ytk,
                      reads=[ytk + 'a', ytk + 'b'], writes=[], out=True)
        ple_part(0)
        for t in range(NT):
            if t + 1 < NT:
                ple_part(t + 1)
            fin_part(t)
        P.barrier()

    for sq_i in range(nseq):
        run_seq(sq_i)
    P.emit()
    es.close()
    return nc


def _host_inputs(x, p, positions, ln_ffn1, w1_gate, w1_up, w1_down, ln_mix, w_in, hg_lb_logits, hg_out_norm,
                 q_a_norm, w_q_up, kv_a_norm, w_kv_up, w_out, ln_ffn2, w2_gate, w2_up, w2_down, ln_ple,
                 w_ple_gate, w_ple_proj, ln_final):
    f = lambda a: np.ascontiguousarray(np.asarray(a), dtype=np.float32)
    win = f(w_in)[0]
    kr0 = 2048 + 256 + 128
    winb = np.concatenate([win[:, 2048:2496], win[:, kr0 + 32:kr0 + 64], win[:, kr0:kr0 + 32]], axis=1)
    wq = f(w_q_up)[0]
    perm = []
    for h in range(4):
        b = h * 192 + 128
        perm += list(range(b + 32, b + 64)) + list(range(b, b + 32))
    wq_full = np.concatenate([wq, wq[:, perm]], axis=1)
    wkv = f(w_kv_up)[0].reshape(128, 4, 256)
    col = lambda v: np.asarray(v, np.float32).reshape(-1, 128).T
    gains = np.concatenate([col(ln_ffn1[0]), col(ln_mix[0]), col(ln_ffn2[0]), col(ln_ple[0]), col(ln_final),
                            col(q_a_norm[0]), col(kv_a_norm[0]), col(np.asarray(hg_out_norm)[0])], axis=1)
    assert gains.shape == (128, 47)
    idx = np.arange(128)
    bd = ((idx[:, None] // 64 == idx[None, :] // 64) & (idx[:, None] <= idx[None, :])).astype(np.float32)
    cm = (idx[None, :] >= idx[:, None]).astype(np.float32)
    ci = (idx[:, None] // 64 == np.arange(2)[None, :]).astype(np.float32)
    misc = np.zeros((128, 16), np.float32)
    misc[:, M_EPS] = EPS
    misc[:, M_ONE] = 1.0
    half = 32
    inv_freq = (np.float32(10000.0) ** (-np.arange(half, dtype=np.float32) / np.float32(half))).astype(np.float32)
    misc[0:64, M_INVF] = np.concatenate([inv_freq, inv_freq])
    misc[0:32, M_SIGN] = -1.0
    misc[32:64, M_SIGN] = 1.0
    cst = np.concatenate([np.eye(128, dtype=np.float32), bd, cm, np.ones((128, 128), np.float32), ci, misc], axis=1)
    assert cst.shape == (128, 530), cst.shape
    cst = np.ascontiguousarray(cst)
    shared = {
        "w1g": f(w1_gate)[0], "w1u": f(w1_up)[0], "w1d": f(w1_down)[0],
        "w2g": f(w2_gate)[0], "w2u": f(w2_up)[0], "w2d": f(w2_down)[0],
        "wina": np.ascontiguousarray(win[:, 0:2048]), "winb": np.ascontiguousarray(winb),
        "wq": np.ascontiguousarray(wq_full),
        "wkvk": np.ascontiguousarray(wkv[:, :, :128].reshape(128, 512)),
        "wkvv": np.ascontiguousarray(wkv[:, :, 128:].reshape(128, 512)),
        "wot": np.ascontiguousarray(f(w_out)[0][0:512]), "wob": np.ascontiguousarray(f(w_out)[0][512:1024]),
        "wpg": f(w_ple_gate)[0], "wpp": f(w_ple_proj)[0],
        "gains": np.ascontiguousarray(gains), "lbl": f(hg_lb_logits), "cst": cst,
    }
    xx = f(x)
    pp = f(p)[0]
    pos = np.ascontiguousarray(np.asarray(positions), dtype=np.int32)
    return shared, xx, pp, pos


def kernel(**inputs):
    shared, xx, pp, pos = _host_inputs(**inputs)
    ncores = 8
    stop = os.environ.get("MK_STOP") or None
    nc = build_program(stop_after=stop)
    in_maps = []
    for c in range(ncores):
        m = dict(shared)
        m["x"] = np.ascontiguousarray(xx[2 * c:2 * c + 2])
        m["p"] = np.ascontiguousarray(pp[2 * c:2 * c + 2])
        m["pos"] = np.ascontiguousarray(pos[2 * c:2 * c + 2])
        in_maps.append(m)
    res = run_bass_kernel_spmd(nc, in_maps, core_ids=list(range(ncores)))
    return np.concatenate([np.asarray(r["y"]) for r in res.results], axis=0).astype(np.float32)
```

```python
import contextlib
import os
import numpy as np
import concourse.bass as bass
import concourse.mybir as mybir
from concourse.bass_utils import run_bass_kernel_spmd

F32 = mybir.dt.float32
BF16 = mybir.dt.bfloat16
I32 = mybir.dt.int32
AF = mybir.ActivationFunctionType
ALU = mybir.AluOpType

D = 1024
S = 2048
NT = 4
DFF = 2816
NHC = 22
EPS = 1e-6
SCALE = 192 ** -0.5
PI = float(np.pi)
C1 = 6.28125
C2 = float(2 * np.pi - 6.28125)
HG_LEVEL = int(os.environ.get('HG_LEVEL', '2'))
MAXOPS = int(os.environ.get('MK_MAXOPS', '100000000'))

G_FFN1, G_MIX, G_FFN2, G_PLE, G_FIN, G_QA, G_KVA, G_HG = 0, 8, 16, 24, 32, 40, 42, 43
K_ID, K_BD, K_CM, K_ON, K_CI, K_MISC = 0, 128, 256, 384, 512, 514
M_EPS, M_ONE, M_INVF, M_SIGN, M_ZERO = 0, 1, 2, 3, 4


class Prog:
    ENGS = ('pe', 'act', 'dve', 'pool', 'sp')

    def __init__(self, nc):
        self.nc = nc
        self.ops = {e: [] for e in self.ENGS}
        self.cnt = {e: 0 for e in self.ENGS}
        self.lastw = {}
        self.readers = {}
        self.dma_cnt = {}
        self.seen = {e: {} for e in self.ENGS}
        self.semnames = set('E:' + e for e in self.ENGS)
        self.out_tokens = []
        self.alias = {}

    def _x(self, keys):
        out = []
        for k in keys:
            out.extend(self.alias.get(k, (k,)))
        return out

    def _deps(self, eng, reads, writes):
        toks = []
        for k in reads:
            t = self.lastw.get(k)
            if t is not None:
                toks.append(t)
        for k in writes:
            t = self.lastw.get(k)
            if t is not None:
                toks.append(t)
            toks.extend(self.readers.get(k, ()))
        need = {}
        for (sem, val, teng) in toks:
            if teng == 'pe' and eng == 'pe':
                continue
            if need.get(sem, 0) < val:
                need[sem] = val
        waits = []
        for sem, val in need.items():
            if self.seen[eng].get(sem, 0) >= val:
                continue
            self.seen[eng][sem] = val
            waits.append((sem, val))
        return waits

    def _commit(self, tok, reads, writes):
        for k in reads:
            self.readers.setdefault(k, []).append(tok)
        for k in writes:
            self.lastw[k] = tok
            self.readers[k] = []

    def op(self, eng, fn, reads=(), writes=()):
        self.total = getattr(self, 'total', 0) + 1
        if self.total > MAXOPS:
            return None
        reads, writes = self._x(reads), self._x(writes)
        waits = self._deps(eng, reads, writes)
        self.cnt[eng] += 1
        tok = ('E:' + eng, self.cnt[eng], eng)
        self.ops[eng].append(('c', waits, fn, tok))
        self._commit(tok, reads, writes)
        return tok

    def dma(self, eng, fn, n, semkey, reads=(), writes=(), out=False):
        self.total = getattr(self, 'total', 0) + 1
        if self.total > MAXOPS:
            return None
        reads, writes = self._x(reads), self._x(writes)
        waits = self._deps(eng, reads, writes)
        sem = 'D:' + semkey
        self.semnames.add(sem)
        self.dma_cnt[sem] = self.dma_cnt.get(sem, 0) + n
        tok = (sem, 16 * self.dma_cnt[sem], 'dma')
        self.ops[eng].append(('d', waits, fn, tok))
        self._commit(tok, reads, writes)
        if out:
            self.out_tokens.append(tok)
        return tok

    def barrier(self):
        allt = {}
        for e in self.ENGS:
            if self.cnt[e]:
                allt['E:' + e] = self.cnt[e]
        for sem, c in self.dma_cnt.items():
            allt[sem] = 16 * c
        for e in self.ENGS:
            waits = []
            for sem, val in allt.items():
                if sem == 'E:' + e:
                    continue
                if self.seen[e].get(sem, 0) >= val:
                    continue
                self.seen[e][sem] = val
                waits.append((sem, val))
            if waits:
                self.ops[e].append(('w', waits, None, None))

    def emit(self):
        nc = self.nc
        fin = {}
        for (sem, val, _) in self.out_tokens:
            fin[sem] = max(fin.get(sem, 0), val)
        with contextlib.ExitStack() as es:
            sems = {}
            for name in sorted(self.semnames):
                sems[name] = es.enter_context(nc.semaphore(name.replace(':', '_')))
            block = es.enter_context(nc.Block())

            def run(engname, eng):
                for (kind, waits, fn, tok) in self.ops[engname]:
                    for (sem, val) in waits:
                        eng.wait_ge(sems[sem], val)
                    if kind == 'c':
                        fn(eng).then_inc(sems[tok[0]], 1)
                    elif kind == 'd':
                        s = sems[tok[0]]
                        fn(eng, lambda inst, s=s: inst.then_inc(s, 16))
                if engname == 'sp':
                    for sem, val in fin.items():
                        eng.wait_ge(sems[sem], val)

            @block.tensor
            def _(e):
                run('pe', e)

            @block.scalar
            def _(e):
                run('act', e)

            @block.vector
            def _(e):
                run('dve', e)

            @block.gpsimd
            def _(e):
                run('pool', e)

            @block.sync
            def _(e):
                run('sp', e)


class Arena:
    def __init__(self, nc, base=16512, limit=229344):
        self.nc, self.off, self.limit, self.n = nc, base, limit, 0

    def alloc(self, name, shape, dtype):
        esz = 2 if dtype == BF16 else 4
        size = esz
        for s in shape[1:]:
            size *= s
        size = (size + 63) // 64 * 64
        assert self.off + size <= self.limit, (name, self.off, size, self.limit)
        self.n += 1
        t = self.nc.alloc_sbuf_tensor_at(f"{name}_{self.n}", list(shape), dtype, offset=self.off)
        self.off += size
        return t

    def mark(self):
        return self.off

    def reset(self, m):
        self.off = m


def build_program(stop_after=None, nseq=2):
    nc = bass.Bass("TRN2", target_bir_lowering=False)
    dram = lambda name, shape, dt=F32, kind="ExternalInput": nc.dram_tensor(name, list(shape), dt, kind=kind).ap()
    x_d = dram("x", [2, S, D])
    p_d = dram("p", [2, S, 256])
    pos_d = dram("pos", [2, S], I32)
    w1g_d, w1u_d, w1d_d = dram("w1g", [D, DFF]), dram("w1u", [D, DFF]), dram("w1d", [DFF, D])
    w2g_d, w2u_d, w2d_d = dram("w2g", [D, DFF]), dram("w2u", [D, DFF]), dram("w2d", [DFF, D])
    wina_d = dram("wina", [D, 2048])
    winb_d = dram("winb", [D, 512])
    wq_d = dram("wq", [256, 1024])
    wkvk_d = dram("wkvk", [128, 512])
    wkvv_d = dram("wkvv", [128, 512])
    wot_d = dram("wot", [512, D])
    wob_d = dram("wob", [512, D])
    wpg_d = dram("wpg", [D, D])
    wpp_d = dram("wpp", [256, D])
    gains_d = dram("gains", [128, 47])
    lbl_d = dram("lbl", [2, 512])
    cst_d = dram("cst", [128, 530])
    y_d = dram("y", [2, S, D], F32, "ExternalOutput")

    P = Prog(nc)
    A = Arena(nc)
    es = contextlib.ExitStack()
    ps = [es.enter_context(nc.psum_tensor(f"ps{i}", [128, 512], F32)) for i in range(8)]
    PK = [f'ps{i}' for i in range(8)]
    for i in range(8):
        P.alias[f'ps{i}'] = tuple(f'ps{i}_{q}' for q in range(4))

    hT = A.alloc("hT", [128, 8, S], F32)
    cst = A.alloc("cst", [128, 530], F32)
    cstb = A.alloc("cstb", [128, 512], BF16)
    gains = A.alloc("gains", [128, 47], F32)
    LBt = A.alloc("LBt", [128, 512], F32)
    LBm1 = A.alloc("LBm1", [128, 512], F32)
    ident_f = cst[:, K_ID:K_ID + 128]
    bdmask = cst[:, K_BD:K_BD + 128]
    cmaskT = cst[:, K_CM:K_CM + 128]
    chunkind = cst[:, K_CI:K_CI + 2]
    misc = lambda j, n=128: cst[0:n, K_MISC + j:K_MISC + j + 1]
    ident_b = cstb[:, 0:128]
    ones_b = cstb[:, 384:512]
    gcol = lambda j: gains[:, j:j + 1]
    phase_base = A.mark()
    lbl = A.alloc("lbl", [128, 2, 512], F32)

    P.dma('sp', lambda e, inc: (inc(e.dma_start(out=cst[:], in_=cst_d[:, :])),
                                inc(e.dma_start(out=gains[:], in_=gains_d[:, :])),
                                inc(e.dma_start(out=lbl[:].rearrange("p a b -> p (a b)"),
                                                in_=lbl_d.rearrange("a b -> (a b)").partition_broadcast(128)))),
          3, 'setup', writes=['cst', 'gains', 'lbl'])
    P.dma('pool', lambda e, inc: inc(e.dma_start(out=cstb[:], in_=cst_d[:, 0:512])), 1, 'setupb', writes=['cstb'])
    P.op('dve', lambda e: e.tensor_tensor(out=LBt[:], in0=lbl[:, 1, :], in1=lbl[:, 0, :], op=ALU.subtract), reads=['lbl'], writes=['LBt'])
    P.op('act', lambda e: e.activation(out=LBt[:], in_=LBt[:], func=AF.Exp), reads=['LBt'], writes=['LBt'])
    P.op('dve', lambda e: e.tensor_scalar(out=LBt[:], in0=LBt[:], scalar1=1.0, scalar2=None, op0=ALU.add), reads=['LBt'], writes=['LBt'])
    P.op('dve', lambda e: e.reciprocal(out=LBt[:], in_=LBt[:]), reads=['LBt'], writes=['LBt'])
    P.op('dve', lambda e: e.tensor_scalar(out=LBm1[:], in0=LBt[:], scalar1=-1.0, scalar2=None, op0=ALU.add), reads=['LBt'], writes=['LBm1'])

    P.barrier()

    hk = lambda c, t: f'h{c}_{t}'
    xk = lambda c, t: f'xn{c}_{t}'
    tsl = lambda t: slice(t * 512, (t + 1) * 512)

    def rms_norm_tile(t, gbase, xn, sq, lnv, psn, out_f32=False):
        for c in range(8):
            P.op('act', lambda e, c=c: e.activation(out=sq[:, c, :], in_=hT[:, c, tsl(t)], func=AF.Square),
                 reads=[hk(c, t)], writes=[f'sq{c}'])
        for c in range(8):
            P.op('pe', lambda e, c=c: e.matmul(ps[psn][:], lhsT=ones_b, rhs=sq[:, c, :], start=(c == 0), stop=(c == 7)),
                 reads=[f'sq{c}', 'cstb'], writes=[PK[psn]])
        P.op('act', lambda e: e.activation(out=lnv[:], in_=ps[psn][:], func=AF.Ln, scale=1.0 / D, bias=misc(M_EPS)),
             reads=[PK[psn], 'cst'], writes=['lnv'])
        P.op('act', lambda e: e.activation(out=lnv[:], in_=lnv[:], func=AF.Exp, scale=-0.5), reads=['lnv'], writes=['lnv'])
        for c in range(8):
            dst = xn[:, c, :] if out_f32 else xn[:, c, tsl(t)]
            P.op('dve', lambda e, c=c, dst=dst: e.scalar_tensor_tensor(out=dst, in0=hT[:, c, tsl(t)], scalar=gcol(gbase + c),
                                                                        in1=lnv[:], op0=ALU.mult, op1=ALU.mult),
                 reads=[hk(c, t), 'lnv', 'gains'], writes=[f'yn{c}' if out_f32 else xk(c, t)])

    def load_w_rows(dst, src, nk, ncols, semkey, wkey, col0=0, eng='pool'):
        pieces = []
        for k in range(nk):
            for c0 in range(0, ncols, 1024):
                cn = min(1024, ncols - c0)
                pieces.append((k, c0, cn))

        def fn(e, inc):
            for (k, c0, cn) in pieces:
                inc(e.dma_start(out=dst[:, k, c0:c0 + cn], in_=src[k * 128:(k + 1) * 128, col0 + c0:col0 + c0 + cn]))
        P.dma(eng, fn, len(pieces), semkey, writes=[wkey])

    def run_seq(sq_i):
        A.reset(phase_base)
        xtok = [A.alloc("xtok", [128, D], F32) for _ in range(4)]
        for blk in range(16):
            xt = xtok[blk % 4]
            t = blk // 4
            P.dma('sp', lambda e, inc, xt=xt, blk=blk: inc(e.dma_start(out=xt[:], in_=x_d[sq_i, blk * 128:(blk + 1) * 128, :])),
                  1, f'xtok{blk % 4}', writes=[f'xtok{blk % 4}'])
            for half in range(2):
                bank = (blk * 2 + half) % 4
                for cc in range(4):
                    c = half * 4 + cc
                    P.op('pe', lambda e, xt=xt, c=c, cc=cc, bank=bank: e.transpose(out=ps[bank][:, cc * 128:(cc + 1) * 128],
                                                                                   in_=xt[:, c * 128:(c + 1) * 128], identity=ident_f),
                         reads=[f'xtok{blk % 4}', 'cst'], writes=[PK[bank]])
                dst = hT[:, half * 4:half * 4 + 4, blk * 128:(blk + 1) * 128]
                src = ps[bank][:].rearrange("p (a b) -> p a b", a=4)
                wk = [hk(half * 4 + cc, t) for cc in range(4)]
                if half == 0:
                    P.op('act', lambda e, dst=dst, src=src: e.activation(out=dst, in_=src, func=AF.Copy), reads=[PK[bank]], writes=wk)
                else:
                    P.op('dve', lambda e, dst=dst, src=src: e.tensor_copy(out=dst, in_=src), reads=[PK[bank]], writes=wk)
        P.barrier()

        def ffn(gbase, wg_d, wu_d, wd_d):
            A.reset(phase_base)
            xn = A.alloc("xn", [128, 8, S], BF16)
            hmid = A.alloc("hmid", [128, 11, S], BF16)
            wgu = [A.alloc("wgu", [128, 2, 8, 256], BF16) for _ in range(2)]
            wd = A.alloc("wd", [128, 11, D], BF16)
            sq = A.alloc("sq", [128, 8, 512], BF16)
            lnv = A.alloc("lnv", [128, 512], F32)
            sg = [A.alloc("sg", [128, 512], BF16) for _ in range(2)]
            for t in range(NT):
                rms_norm_tile(t, gbase, xn, sq, lnv, 6)
            groups = []
            for hh in range(2):
                for (j0, n) in [(0, 2), (2, 2), (4, 2), (6, 2), (8, 2), (10, 1)]:
                    groups.append((hh, j0, n))

            def load_group(gi):
                hh, j0, n = groups[gi]
                slot = gi % 2
                col0 = (hh * 11 + j0) * 128

                def fn(e, inc):
                    for k in range(8):
                        inc(e.dma_start(out=wgu[slot][:, 0, k, 0:n * 128], in_=wg_d[k * 128:(k + 1) * 128, col0:col0 + n * 128]))
                        inc(e.dma_start(out=wgu[slot][:, 1, k, 0:n * 128], in_=wu_d[k * 128:(k + 1) * 128, col0:col0 + n * 128]))
                P.dma('pool', fn, 16, f'wgu{slot}', writes=[f'wgu{slot}'])

            def load_wd(hh):
                for j in range(11):
                    r0 = (hh * 11 + j) * 128
                    P.dma('pool', lambda e, inc, j=j, r0=r0: inc(e.dma_start(out=wd[:, j, :], in_=wd_d[r0:r0 + 128, :])),
                          1, f'wd{j}', writes=[f'wd{j}'])

            cnt = 0
            cnt2 = 0
            load_group(0)
            for gi, (hh, j0, n) in enumerate(groups):
                if gi + 1 < len(groups):
                    load_group(gi + 1)
                if j0 == 0:
                    load_wd(hh)
                slot = gi % 2
                for jj in range(n):
                    j = j0 + jj
                    for t in range(NT):
                        bg, bu = cnt % 2, 2 + cnt % 2
                        for k in range(8):
                            P.op('pe', lambda e, k=k, jj=jj, t=t, bg=bg, slot=slot: e.matmul(
                                ps[bg][:], lhsT=wgu[slot][:, 0, k, jj * 128:(jj + 1) * 128], rhs=xn[:, k, tsl(t)], start=(k == 0), stop=(k == 7)),
                                reads=[f'wgu{slot}', xk(k, t)], writes=[PK[bg]])
                        for k in range(8):
                            P.op('pe', lambda e, k=k, jj=jj, t=t, bu=bu, slot=slot: e.matmul(
                                ps[bu][:], lhsT=wgu[slot][:, 1, k, jj * 128:(jj + 1) * 128], rhs=xn[:, k, tsl(t)], start=(k == 0), stop=(k == 7)),
                                reads=[f'wgu{slot}', xk(k, t)], writes=[PK[bu]])
                        sgt = sg[cnt % 2]
                        P.op('act', lambda e, bg=bg, sgt=sgt: e.activation(out=sgt[:], in_=ps[bg][:], func=AF.Silu),
                             reads=[PK[bg]], writes=[f'sg{cnt % 2}'])
                        P.op('dve', lambda e, bu=bu, sgt=sgt, j=j, t=t: e.tensor_tensor(out=hmid[:, j, tsl(t)], in0=sgt[:], in1=ps[bu][:], op=ALU.mult),
                             reads=[PK[bu], f'sg{cnt % 2}'], writes=[f'hm{j}_{t}'])
                        cnt += 1
                if j0 == 10:
                    for t in range(NT):
                        for o in range(8):
                            bd = 4 + cnt2 % 2
                            for j in range(11):
                                P.op('pe', lambda e, j=j, o=o, t=t, bd=bd: e.matmul(
                                    ps[bd][:], lhsT=wd[:, j, o * 128:(o + 1) * 128], rhs=hmid[:, j, tsl(t)], start=(j == 0), stop=(j == 10)),
                                    reads=[f'wd{j}', f'hm{j}_{t}'], writes=[PK[bd]])
                            P.op('dve', lambda e, o=o, t=t, bd=bd: e.scalar_tensor_tensor(
                                out=hT[:, o, tsl(t)], in0=ps[bd][:], scalar=0.5, in1=hT[:, o, tsl(t)], op0=ALU.mult, op1=ALU.add),
                                reads=[PK[bd], hk(o, t)], writes=[hk(o, t)])
                            cnt2 += 1
            P.barrier()

        if stop_after != 'x':
            ffn(G_FFN1, w1g_d, w1u_d, w1d_d)

        if stop_after not in ('x', 'ffn1'):
            A.reset(phase_base)
            xn = A.alloc("xnm", [128, 8, S], BF16)
            mix_base = A.mark()
            sq = A.alloc("sq", [128, 8, 512], BF16)
            lnv = A.alloc("lnv", [128, 512], F32)
            for t in range(NT):
                rms_norm_tile(t, G_MIX, xn, sq, lnv, 6)
            P.barrier()

            A.reset(mix_base)
            Wa = A.alloc("Wa", [128, 8, 2048], BF16)
            Wot = A.alloc("Wot", [128, 4, D], BF16)
            tm = {n: A.alloc(n, [128, 512], F32) for n in ('e', 't1', 'l1', 'l2', 'sg', 'kk', 'ebn', 'ebp')}
            logf = A.alloc("logf", [128, 512], F32)
            Qp_tok = A.alloc("Qp_tok", [128, 512], BF16)
            Kp_tok = A.alloc("Kp_tok", [128, 4, 512], BF16)
            V_tok = A.alloc("V_tok", [128, 4, 512], BF16)
            QpT = A.alloc("QpT", [128, 4, 512], BF16)
            KpT = A.alloc("KpT", [128, 4, 512], BF16)
            sgT = A.alloc("sgT", [128, 4, 512], BF16)
            AT = [A.alloc("AT", [128, 128], BF16) for _ in range(4)]
            S32 = A.alloc("S32", [128, 4, 128], F32)
            S32e = A.alloc("S32e", [128, 4, 128], F32)
            Sbf = A.alloc("Sbf", [128, 4, 128], BF16)
            Ecol = A.alloc("Ecol", [128, 32], F32)
            sqo = A.alloc("sqo", [128, 512], BF16)
            lnvo = A.alloc("lnvo", [128, 512], F32)
            to = A.alloc("to", [128, 512], F32)
            ohg = A.alloc("ohg", [128, 4, 512], BF16)
            load_w_rows(Wa, wina_d, 8, 2048, 'Wa', 'Wa')
            load_w_rows(Wot, wot_d, 4, D, 'Wot', 'Wot')
            P.op('dve', lambda e: e.memset(S32[:], 0.0), writes=[f'S32_{h}' for h in range(4)])
            P.op('pool', lambda e: e.memset(Sbf[:], 0.0), writes=[f'Sbf_{h}' for h in range(4)])
            psTb = ps[4][:].bitcast(BF16)
            psTk = ps[5][:].bitcast(BF16)

            def tile_a(t):
                e_, t1, l1, l2, sg_, kk, ebn, ebp = (tm[n] for n in ('e', 't1', 'l1', 'l2', 'sg', 'kk', 'ebn', 'ebp'))
                QB = [1, 7]

                def proj(blk):
                    r = slice(t * 512 + blk * 128, t * 512 + (blk + 1) * 128)
                    for (bank, col0) in ((0, 512), (2, 1024), (QB[blk % 2], 0)):
                        for k in range(8):
                            P.op('pe', lambda e, k=k, bank=bank, col0=col0: e.matmul(
                                ps[bank][:], lhsT=xn[:, k, r], rhs=Wa[:, k, col0:col0 + 512], start=(k == 0), stop=(k == 7)),
                                reads=[xk(k, t), 'Wa'], writes=[PK[bank]])

                def chain1(blk):
                    P.op('act', lambda e: e.activation(out=e_[:], in_=ps[0][:], func=AF.Exp, scale=-1.0), reads=[PK[0]], writes=['e'])
                    P.op('act', lambda e: e.activation(out=V_tok[:, blk, :], in_=ps[2][:], func=AF.Copy), reads=[PK[2]], writes=[f'V_tok{blk}'])
                    P.op('dve', lambda e: e.tensor_tensor(out=t1[:], in0=e_[:], in1=LBt[:], op=ALU.mult), reads=['e', 'LBt'], writes=['t1'])
                    P.op('act', lambda e: e.activation(out=l2[:], in_=e_[:], func=AF.Ln, bias=misc(M_ONE)), reads=['e', 'cst'], writes=['l2'])
                    P.op('act', lambda e: e.activation(out=l1[:], in_=t1[:], func=AF.Ln, bias=misc(M_ONE)), reads=['t1', 'cst'], writes=['l1'])
                    P.op('dve', lambda e: e.tensor_tensor(out=logf[:], in0=l1[:], in1=l2[:], op=ALU.subtract), reads=['l1', 'l2'], writes=['logf'])
                    P.op('act', lambda e: e.activation(out=sg_[:], in_=l2[:], func=AF.Exp, scale=-1.0), reads=['l2'], writes=['sg'])
                    P.op('dve', lambda e: e.scalar_tensor_tensor(out=kk[:], in0=sg_[:], scalar=-1.0, in1=LBm1[:], op0=ALU.add, op1=ALU.mult),
                         reads=['sg', 'LBm1'], writes=['kk'])

                def cumsum(blk):
                    P.op('pe', lambda e: e.matmul(ps[3][:], lhsT=bdmask, rhs=logf[:], start=True, stop=True), reads=['cst', 'logf'], writes=[PK[3]])

                def chain2a(blk):
                    qb = QB[blk % 2]
                    for h in range(4):
                        P.op('pe', lambda e, h=h: e.matmul(ps[6][:, h * 8 + blk * 2:h * 8 + blk * 2 + 2], lhsT=logf[:, h * 128:(h + 1) * 128],
                                                           rhs=chunkind, start=True, stop=True),
                             reads=['logf', 'cst'], writes=[PK[6]])
                    P.op('act', lambda e: e.activation(out=ebn[:], in_=ps[3][:], func=AF.Exp, scale=-1.0), reads=[PK[3]], writes=['ebn'])
                    P.op('act', lambda e: e.activation(out=ebp[:], in_=ps[3][:], func=AF.Exp), reads=[PK[3]], writes=['ebp'])
                    P.op('dve', lambda e: e.tensor_tensor(out=Kp_tok[:, blk, :], in0=kk[:], in1=ebn[:], op=ALU.mult),
                         reads=['kk', 'ebn'], writes=[f'Kp_tok{blk}'])
                    P.op('dve', lambda e: e.tensor_tensor(out=Qp_tok[:], in0=ps[qb][:], in1=ebp[:], op=ALU.mult), reads=[PK[qb], 'ebp'], writes=['Qp_tok'])

                def chain2b(blk):
                    for h in range(4):
                        P.op('pe', lambda e, h=h: e.transpose(out=psTk[:, h * 128:(h + 1) * 128],
                                                              in_=Kp_tok[:, blk, h * 128:(h + 1) * 128], identity=ident_b),
                             reads=[f'Kp_tok{blk}', 'cstb'], writes=[PK[5]])
                    for h in range(4):
                        P.op('pe', lambda e, h=h: e.transpose(out=psTb[:, h * 128:(h + 1) * 128], in_=Qp_tok[:, h * 128:(h + 1) * 128], identity=ident_b),
                             reads=['Qp_tok', 'cstb'], writes=[PK[4]])
                    bs = slice(blk * 128, (blk + 1) * 128)
                    P.op('act', lambda e: e.activation(out=KpT[:, :, bs], in_=psTk[:, 0:512].rearrange("p (a b) -> p a b", a=4), func=AF.Copy),
                         reads=[PK[5]], writes=[f'KpT{blk}'])
                    P.op('dve', lambda e: e.tensor_copy(out=QpT[:, :, bs], in_=psTb[:, 0:512].rearrange("p (a b) -> p a b", a=4)),
                         reads=[PK[4]], writes=[f'QpT{blk}'])

                proj(0)
                chain1(0)
                for blk in range(4):
                    if blk + 1 < 4:
                        proj(blk + 1)
                    cumsum(blk)
                    chain2a(blk)
                    if blk + 1 < 4:
                        chain1(blk + 1)
                    chain2b(blk)
                P.op('act', lambda e: e.activation(out=Ecol[:], in_=ps[6][:, 0:32], func=AF.Exp), reads=[PK[6]], writes=['Ecol'])
                for h in range(4):
                    for k in range(8):
                        P.op('pe', lambda e, k=k, h=h: e.matmul(ps[5][:], lhsT=Wa[:, k, 1536 + h * 128:1536 + (h + 1) * 128], rhs=xn[:, k, tsl(t)],
                                                                start=(k == 0), stop=(k == 7)),
                             reads=[xk(k, t), 'Wa'], writes=[PK[5]])
                    P.op('act', lambda e, h=h: e.activation(out=sgT[:, h, :], in_=ps[5][:], func=AF.Silu), reads=[PK[5]], writes=[f'sgT{h}'])
                if HG_LEVEL < 1:
                    return
                HS = [slice(h * 128, (h + 1) * 128) for h in range(4)]
                KVB = [7, 4, 5, 7]

                def state_update(blk, c):
                    for h in range(4):
                        ec = Ecol[:, h * 8 + blk * 2 + c:h * 8 + blk * 2 + c + 1]
                        P.op('act', lambda e, h=h, ec=ec: e.activation(out=S32e[:, h, :], in_=S32[:, h, :], func=AF.Identity, scale=ec),
                             reads=[f'S32_{h}', 'Ecol'], writes=[f'S32e_{h}'])
                        P.op('dve', lambda e, h=h, ec=ec: e.scalar_tensor_tensor(out=S32[:, h, :], in0=ps[KVB[h]][:, HS[h]], scalar=ec, in1=S32e[:, h, :],
                                                                               op0=ALU.mult, op1=ALU.add),
                             reads=[PK[KVB[h]], 'Ecol', f'S32e_{h}'], writes=[f'S32_{h}'])
                        P.op('pool', lambda e, h=h: e.tensor_copy(out=Sbf[:, h, :], in_=S32[:, h, :]), reads=[f'S32_{h}'], writes=[f'Sbf_{h}'])

                def kv_mm(blk, c, h):
                    pr = slice(c * 64, (c + 1) * 64)
                    P.op('pe', lambda e, h=h, pr=pr, blk=blk: e.matmul(ps[KVB[h]][:, HS[h]], lhsT=Kp_tok[pr, blk, HS[h]], rhs=V_tok[pr, blk, HS[h]],
                                                                       start=True, stop=True),
                         reads=[f'Kp_tok{blk}', f'V_tok{blk}'], writes=[PK[KVB[h]]])

                for blk in range(4):
                    bs = slice(blk * 128, (blk + 1) * 128)
                    for h in range(4):
                        P.op('pe', lambda e, h=h, bs=bs: e.matmul(ps[6][:, HS[h]], lhsT=KpT[:, h, bs], rhs=QpT[:, h, bs], start=True, stop=True),
                             reads=[f'KpT{blk}', f'QpT{blk}'], writes=[PK[6]])
                    for h in range(4):
                        P.op('dve', lambda e, h=h: e.tensor_tensor(out=AT[h][:], in0=ps[6][:, HS[h]], in1=bdmask, op=ALU.mult),
                             reads=[PK[6], 'cst'], writes=[f'AT{h}'])
                    for h in range(4):
                        cs = slice(blk * 128, blk * 128 + 64)
                        P.op('pe', lambda e, h=h, bs=bs, blk=blk: e.matmul(ps[h][:, bs], lhsT=V_tok[:, blk, HS[h]], rhs=AT[h][:], start=True, stop=False),
                             reads=[f'V_tok{blk}', f'AT{h}'], writes=[PK[h]])
                        P.op('pe', lambda e, h=h, cs=cs: e.matmul(ps[h][:, cs], lhsT=Sbf[:, h, :], rhs=QpT[:, h, cs], start=False, stop=False),
                             reads=[f'Sbf_{h}', f'QpT{blk}'], writes=[PK[h]])
                        kv_mm(blk, 0, h)
                    state_update(blk, 0)
                    for h in range(4):
                        cs = slice(blk * 128 + 64, blk * 128 + 128)
                        P.op('pe', lambda e, h=h, cs=cs: e.matmul(ps[h][:, cs], lhsT=Sbf[:, h, :], rhs=QpT[:, h, cs], start=False, stop=True),
                             reads=[f'Sbf_{h}', f'QpT{blk}'], writes=[PK[h]])
                        kv_mm(blk, 1, h)
                    state_update(blk, 1)
                if HG_LEVEL < 2:
                    return
                lnv4 = [(tm[n], n) for n in ('e', 't1', 'l1', 'l2')]
                sq4 = [(tm[n][:].bitcast(BF16)[:, 0:512], n) for n in ('sg', 'kk', 'ebn', 'ebp')]
                for h in range(4):
                    sqa, sqk = sq4[h]
                    P.op('act', lambda e, h=h, sqa=sqa: e.activation(out=sqa, in_=ps[h][:], func=AF.Square), reads=[PK[h]], writes=[sqk])
                for h in range(4):
                    sqa, sqk = sq4[h]
                    P.op('pe', lambda e, h=h, sqa=sqa: e.matmul(ps[4 + h][:], lhsT=ones_b, rhs=sqa, start=True, stop=True), reads=[sqk, 'cstb'], writes=[PK[4 + h]])
                for h in range(4):
                    lv, lk = lnv4[h]
                    P.op('act', lambda e, h=h, lv=lv: e.activation(out=lv[:], in_=ps[4 + h][:], func=AF.Ln, scale=1.0 / 128, bias=misc(M_EPS)),
                         reads=[PK[4 + h], 'cst'], writes=[lk])
                for h in range(4):
                    lv, lk = lnv4[h]
                    P.op('act', lambda e, lv=lv: e.activation(out=lv[:], in_=lv[:], func=AF.Exp, scale=-0.5), reads=[lk], writes=[lk])
                for h in range(4):
                    lv, lk = lnv4[h]
                    P.op('dve', lambda e, h=h, lv=lv: e.scalar_tensor_tensor(out=lv[:], in0=ps[h][:], scalar=gcol(G_HG + h), in1=lv[:], op0=ALU.mult, op1=ALU.mult),
                         reads=[PK[h], 'gains', lk], writes=[lk])
                for h in range(4):
                    lv, lk = lnv4[h]
                    P.op('dve', lambda e, h=h, lv=lv: e.tensor_tensor(out=ohg[:, h, :], in0=lv[:], in1=sgT[:, h, :], op=ALU.mult),
                         reads=[lk, f'sgT{h}'], writes=[f'ohg{h}'])
                for o in range(8):
                    bank = 5 if o % 2 == 0 else 7
                    for h in range(4):
                        P.op('pe', lambda e, o=o, h=h, bank=bank: e.matmul(ps[bank][:], lhsT=Wot[:, h, o * 128:(o + 1) * 128], rhs=ohg[:, h, :],
                                                                           start=(h == 0), stop=(h == 3)),
                             reads=['Wot', f'ohg{h}'], writes=[PK[bank]])
                    P.op('dve', lambda e, o=o, bank=bank: e.tensor_tensor(out=hT[:, o, tsl(t)], in0=ps[bank][:], in1=hT[:, o, tsl(t)], op=ALU.add),
                         reads=[PK[bank], hk(o, t)], writes=[hk(o, t)])
            for t in range(NT):
                tile_a(t)
            P.barrier()

            if stop_after != 'hgrn':
                A.reset(mix_base)
                Wb = A.alloc("Wb", [128, 8, 512], BF16)
                Wq = A.alloc("Wq", [128, 2, 1024], BF16)
                Wkk = A.alloc("Wkk", [128, 1, 512], BF16)
                Wkv = A.alloc("Wkv", [128, 1, 512], BF16)
                Wob = A.alloc("Wob", [128, 4, D], BF16)
                knT = A.alloc("knT", [128, 4, S], BF16)
                krT = A.alloc("krT", [128, S], BF16)
                Vc = A.alloc("Vc", [128, 16, 512], BF16)
                cq = A.alloc("cq", [128, 3, 512], F32)
                sqb = A.alloc("sqb", [128, 3, 512], BF16)
                lnq = A.alloc("lnq", [128, 512], F32)
                cqn = A.alloc("cqn", [128, 3, 512], BF16)
                qnT = A.alloc("qnT", [128, 4, 512], BF16)
                qrT = A.alloc("qrT", [128, 4, 512], BF16)
                posi = A.alloc("posi", [64, 512], I32)
                ang = A.alloc("ang", [64, 512], F32)
                ni = posi
                nf = A.alloc("nf", [64, 512], F32)
                rr = A.alloc("rr", [64, 512], F32)
                mm_ = A.alloc("mm", [64, 512], F32)
                Ct = A.alloc("Ct", [64, 512], F32)
                St = A.alloc("St", [64, 512], F32)
                ra, rb = rr, mm_
                PT = [A.alloc("PT", [128, 512], BF16) for _ in range(3)]
                rden = lnq
                omla = A.alloc("omla", [128, 4, 512], BF16)
                load_w_rows(Wb, winb_d, 8, 512, 'Wb', 'Wb')
                load_w_rows(Wq, wq_d, 2, 1024, 'Wq', 'Wq')
                load_w_rows(Wkk, wkvk_d, 1, 512, 'Wkk', 'Wkk')
                load_w_rows(Wkv, wkvv_d, 1, 512, 'Wkv', 'Wkv')
                load_w_rows(Wob, wob_d, 4, D, 'Wob', 'Wob')
                P.op('pool', lambda e: e.memset(krT[64:65, :], 1.0), writes=['krT_one'])
                P.op('pool', lambda e: e.memset(qrT[64:65, :, :], 0.0), writes=['qrT_one'])

                def range_reduce(dst, shift, key):
                    P.op('dve', lambda e: e.tensor_scalar(out=rr[:], in0=ang[:], scalar1=shift, scalar2=None, op0=ALU.add), reads=['ang'], writes=['rr'])
                    P.op('dve', lambda e: e.tensor_scalar(out=ni[:], in0=rr[:], scalar1=float(1.0 / (2 * np.pi)), scalar2=None, op0=ALU.mult),
                         reads=['rr'], writes=['posi'])
                    P.op('dve', lambda e: e.tensor_copy(out=nf[:], in_=ni[:]), reads=['posi'], writes=['nf'])
                    P.op('dve', lambda e: e.scalar_tensor_tensor(out=rr[:], in0=nf[:], scalar=-C1, in1=rr[:], op0=ALU.mult, op1=ALU.add),
                         reads=['nf', 'rr'], writes=['rr'])
                    P.op('dve', lambda e: e.scalar_tensor_tensor(out=rr[:], in0=nf[:], scalar=-C2, in1=rr[:], op0=ALU.mult, op1=ALU.add),
                         reads=['nf', 'rr'], writes=['rr'])
                    P.op('dve', lambda e: e.tensor_scalar(out=mm_[:], in0=rr[:], scalar1=PI, scalar2=-2 * PI, op0=ALU.is_gt, op1=ALU.mult),
                         reads=['rr'], writes=['mm'])
                    P.op('dve', lambda e: e.tensor_tensor(out=nf[:], in0=rr[:], in1=mm_[:], op=ALU.add), reads=['rr', 'mm'], writes=['nf'])
                    P.op('dve', lambda e: e.tensor_scalar(out=mm_[:], in0=rr[:], scalar1=-PI, scalar2=2 * PI, op0=ALU.is_lt, op1=ALU.mult),
                         reads=['rr'], writes=['mm'])
                    P.op('dve', lambda e: e.tensor_tensor(out=nf[:], in0=nf[:], in1=mm_[:], op=ALU.add), reads=['nf', 'mm'], writes=['nf'])
                    P.op('dve', lambda e: e.tensor_scalar(out=nf[:], in0=nf[:], scalar1=PI, scalar2=-PI, op0=ALU.min, op1=ALU.max), reads=['nf'], writes=['nf'])
                    P.op('act', lambda e: e.activation(out=dst[:], in_=nf[:], func=AF.Sin), reads=['nf'], writes=[key])

                def rope(dst, psa, psb, ka, kb, wkey):
                    P.op('dve', lambda e: e.tensor_tensor(out=ra[:], in0=psa, in1=Ct[:], op=ALU.mult), reads=[ka, 'Ct'], writes=['rr'])
                    P.op('dve', lambda e: e.tensor_tensor(out=rb[:], in0=psb, in1=St[:], op=ALU.mult), reads=[kb, 'St'], writes=['mm'])
                    P.op('dve', lambda e: e.tensor_tensor(out=dst, in0=ra[:], in1=rb[:], op=ALU.add), reads=['rr', 'mm'], writes=[wkey])

                pcnt_box = [0]

                def tile_b(t):
                    P.dma('sp', lambda e, inc, t=t: inc(e.dma_start(out=posi[:], in_=pos_d[sq_i, tsl(t)].partition_broadcast(64))), 1, 'posi', writes=['posi'])
                    P.op('dve', lambda e: e.tensor_copy(out=ang[:], in_=posi[:]), reads=['posi'], writes=['ang'])
                    P.op('dve', lambda e: e.tensor_scalar(out=ang[:], in0=ang[:], scalar1=misc(M_INVF, 64), scalar2=None, op0=ALU.mult),
                         reads=['ang', 'cst'], writes=['ang'])
                    range_reduce(Ct, PI / 2, 'Ct')
                    range_reduce(St, 0.0, 'St')
                    P.op('dve', lambda e: e.tensor_scalar(out=St[:], in0=St[:], scalar1=misc(M_SIGN, 64), scalar2=None, op0=ALU.mult),
                         reads=['St', 'cst'], writes=['St'])
                    for oc in range(3):
                        bank = oc % 2
                        for k in range(8):
                            P.op('pe', lambda e, k=k, oc=oc, bank=bank: e.matmul(ps[bank][:], lhsT=Wb[:, k, oc * 128:(oc + 1) * 128], rhs=xn[:, k, tsl(t)],
                                                                                 start=(k == 0), stop=(k == 7)),
                                 reads=['Wb', xk(k, t)], writes=[PK[bank]])
                        P.op('act', lambda e, oc=oc, bank=bank: e.activation(out=cq[:, oc, :], in_=ps[bank][:], func=AF.Copy), reads=[PK[bank]], writes=[f'cq{oc}'])
                        P.op('act', lambda e, oc=oc, bank=bank: e.activation(out=sqb[:, oc, :], in_=ps[bank][:], func=AF.Square), reads=[PK[bank]], writes=[f'sqb{oc}'])
                    for oc in range(2):
                        P.op('pe', lambda e, oc=oc: e.matmul(ps[2][:], lhsT=ones_b, rhs=sqb[:, oc, :], start=(oc == 0), stop=(oc == 1)),
                             reads=[f'sqb{oc}', 'cstb'], writes=[PK[2]])
                    P.op('act', lambda e: e.activation(out=lnq[:], in_=ps[2][:], func=AF.Ln, scale=1.0 / 256, bias=misc(M_EPS)), reads=[PK[2], 'cst'], writes=['lnq'])
                    P.op('act', lambda e: e.activation(out=lnq[:], in_=lnq[:], func=AF.Exp, scale=-0.5), reads=['lnq'], writes=['lnq'])
                    for oc in range(2):
                        P.op('dve', lambda e, oc=oc: e.scalar_tensor_tensor(out=cqn[:, oc, :], in0=cq[:, oc, :], scalar=gcol(G_QA + oc), in1=lnq[:],
                                                                            op0=ALU.mult, op1=ALU.mult),
                             reads=[f'cq{oc}', 'gains', 'lnq'], writes=[f'cqn{oc}'])
                    P.op('pe', lambda e: e.matmul(ps[2][:], lhsT=ones_b, rhs=sqb[:, 2, :], start=True, stop=True), reads=['sqb2', 'cstb'], writes=[PK[2]])
                    P.op('act', lambda e: e.activation(out=lnq[:], in_=ps[2][:], func=AF.Ln, scale=1.0 / 128, bias=misc(M_EPS)), reads=[PK[2], 'cst'], writes=['lnq'])
                    P.op('act', lambda e: e.activation(out=lnq[:], in_=lnq[:], func=AF.Exp, scale=-0.5), reads=['lnq'], writes=['lnq'])
                    P.op('dve', lambda e: e.scalar_tensor_tensor(out=cqn[:, 2, :], in0=cq[:, 2, :], scalar=gcol(G_KVA), in1=lnq[:], op0=ALU.mult, op1=ALU.mult),
                         reads=['cq2', 'gains', 'lnq'], writes=['cqn2'])
                    for (bank, c0) in ((0, 384), (1, 448)):
                        for k in range(8):
                            P.op('pe', lambda e, k=k, bank=bank, c0=c0: e.matmul(ps[bank][0:64, :], lhsT=Wb[:, k, c0:c0 + 64], rhs=xn[:, k, tsl(t)],
                                                                                 start=(k == 0), stop=(k == 7)),
                                 reads=['Wb', xk(k, t)], writes=[PK[bank]])
                    rope(krT[0:64, tsl(t)], ps[0][0:64, :], ps[1][0:64, :], PK[0], PK[1], f'krT{t}')
                    for h in range(4):
                        for k2 in range(2):
                            P.op('pe', lambda e, k2=k2, h=h: e.matmul(ps[2][:], lhsT=Wq[:, k2, h * 192:h * 192 + 128], rhs=cqn[:, k2, :], start=(k2 == 0), stop=(k2 == 1)),
                                 reads=['Wq', f'cqn{k2}'], writes=[PK[2]])
                        P.op('act', lambda e, h=h: e.activation(out=qnT[:, h, :], in_=ps[2][:], func=AF.Copy), reads=[PK[2]], writes=[f'qnT{h}'])
                        for (bank, c0) in ((0, h * 192 + 128), (1, 768 + h * 64)):
                            for k2 in range(2):
                                P.op('pe', lambda e, k2=k2, bank=bank, c0=c0: e.matmul(ps[bank][0:64, :], lhsT=Wq[:, k2, c0:c0 + 64], rhs=cqn[:, k2, :],
                                                                                       start=(k2 == 0), stop=(k2 == 1)),
                                     reads=['Wq', f'cqn{k2}'], writes=[PK[bank]])
                        rope(qrT[0:64, h, :], ps[0][0:64, :], ps[1][0:64, :], PK[0], PK[1], f'qrT{h}')
                        P.op('pe', lambda e, h=h: e.matmul(ps[2][:], lhsT=Wkk[:, 0, h * 128:(h + 1) * 128], rhs=cqn[:, 2, :], start=True, stop=True),
                             reads=['Wkk', 'cqn2'], writes=[PK[2]])
                        P.op('act', lambda e, h=h: e.activation(out=knT[:, h, tsl(t)], in_=ps[2][:], func=AF.Copy), reads=[PK[2]], writes=[f'knT{h}_{t}'])
                    for blk in range(4):
                        bank = blk % 2
                        P.op('pe', lambda e, blk=blk, bank=bank: e.matmul(ps[bank][:], lhsT=cqn[:, 2, blk * 128:(blk + 1) * 128], rhs=Wkv[:, 0, :], start=True, stop=True),
                             reads=['Wkv', 'cqn2'], writes=[PK[bank]])
                        P.op('dve', lambda e, blk=blk, bank=bank: e.tensor_copy(out=Vc[:, t * 4 + blk, :], in_=ps[bank][:]), reads=[PK[bank]], writes=[f'Vc{t * 4 + blk}'])
                    nkb = 4 * t + 4
                    steps = [(h, kb) for h in range(4) for kb in range(nkb)]
                    LA = 2

                    def qk_step(idx):
                        h, kb = steps[idx]
                        i = kb - 4 * t
                        q0 = 0 if i < 0 else i * 128
                        qs = slice(q0, 512)
                        ks = slice(kb * 128, (kb + 1) * 128)
                        pcnt = pcnt_box[0] + idx
                        pb = 3 + pcnt % 3
                        pt = PT[pcnt % 3]
                        ptk = f'PT{pcnt % 3}'
                        tk = kb // 4
                        P.op('pe', lambda e: e.matmul(ps[pb][:, qs], lhsT=knT[:, h, ks], rhs=qnT[:, h, qs], start=True, stop=False),
                             reads=[f'knT{h}_{tk}', f'qnT{h}'], writes=[PK[pb]])
                        P.op('pe', lambda e: e.matmul(ps[pb][:, qs], lhsT=krT[0:65, ks], rhs=qrT[0:65, h, qs], start=False, stop=True),
                             reads=[f'krT{tk}', 'krT_one', f'qrT{h}', 'qrT_one'], writes=[PK[pb]])
                        P.op('act', lambda e: e.activation(out=pt[:, qs], in_=ps[pb][:, qs], func=AF.Exp, scale=SCALE),
                             reads=[PK[pb]], writes=[ptk])
                        if i >= 0:
                            ds_ = slice(q0, q0 + 128)
                            P.op('pool', lambda e: e.tensor_tensor(out=pt[:, ds_], in0=pt[:, ds_], in1=cmaskT, op=ALU.mult),
                                 reads=[ptk, 'cst'], writes=[ptk])

                    def pv_step(idx):
                        h, kb = steps[idx]
                        i = kb - 4 * t
                        q0 = 0 if i < 0 else i * 128
                        qs = slice(q0, 512)
                        pcnt = pcnt_box[0] + idx
                        pt = PT[pcnt % 3]
                        ptk = f'PT{pcnt % 3}'
                        bo, bd_ = (6, 7) if h % 2 == 0 else (0, 1)
                        P.op('pe', lambda e: e.matmul(ps[bo][:, qs], lhsT=Vc[:, kb, h * 128:(h + 1) * 128], rhs=pt[:, qs],
                                                      start=(kb == 0), stop=(kb == nkb - 1)),
                             reads=[f'Vc{kb}', ptk], writes=[PK[bo]])
                        P.op('pe', lambda e: e.matmul(ps[bd_][:, qs], lhsT=ones_b, rhs=pt[:, qs], start=(kb == 0), stop=(kb == nkb - 1)),
                             reads=['cstb', ptk], writes=[PK[bd_]])
                        if kb == nkb - 1:
                            P.op('dve', lambda e: e.reciprocal(out=rden[:], in_=ps[bd_][:]), reads=[PK[bd_]], writes=['lnq'])
                            P.op('dve', lambda e: e.tensor_tensor(out=omla[:, h, :], in0=ps[bo][:], in1=rden[:], op=ALU.mult),
                                 reads=[PK[bo], 'lnq'], writes=[f'omla{h}'])

                    for idx in range(len(steps) + LA):
                        if idx < len(steps):
                            qk_step(idx)
                        if idx - LA >= 0:
                            pv_step(idx - LA)
                    pcnt_box[0] += len(steps)
                    for o in range(8):
                        bank = o % 2
                        for h in range(4):
                            P.op('pe', lambda e, o=o, h=h, bank=bank: e.matmul(ps[bank][:], lhsT=Wob[:, h, o * 128:(o + 1) * 128], rhs=omla[:, h, :],
                                                                               start=(h == 0), stop=(h == 3)),
                                 reads=['Wob', f'omla{h}'], writes=[PK[bank]])
                        P.op('dve', lambda e, o=o, bank=bank: e.tensor_tensor(out=hT[:, o, tsl(t)], in0=ps[bank][:], in1=hT[:, o, tsl(t)], op=ALU.add),
                             reads=[PK[bank], hk(o, t)], writes=[hk(o, t)])
                for t in range(NT):
                    tile_b(t)
                P.barrier()

        if stop_after is None or stop_after in ('ffn2', 'ple'):
            ffn(G_FFN2, w2g_d, w2u_d, w2d_d)

        A.reset(phase_base)
        xnp = A.alloc("xnp", [128, 8, S], BF16)
        sq = A.alloc("sq", [128, 8, 512], BF16)
        lnv = A.alloc("lnv", [128, 512], F32)
        Wg = A.alloc("Wg", [128, 8, D], BF16)
        Wp = A.alloc("Wp", [128, 2, D], BF16)
        ptok = [A.alloc("ptok", [128, 256], F32) for _ in range(2)]
        pT = A.alloc("pT", [128, 2, 512], BF16)
        sig = A.alloc("sig", [128, 512], F32)
        tpl = A.alloc("tpl", [128, 512], F32)
        yn = A.alloc("yn", [128, 8, 512], F32)
        ytok = [A.alloc("ytok", [128, D], F32) for _ in range(2)]
        do_ple = stop_after is None or stop_after == 'ple'
        if do_ple:
            load_w_rows(Wg, wpg_d, 8, D, 'Wg', 'Wg')
            load_w_rows(Wp, wpp_d, 2, D, 'Wp', 'Wp')
        ycnt_box = [0]

        def ple_part(t):
            if do_ple:
                rms_norm_tile(t, G_PLE, xnp, sq, lnv, 6)
                for blk in range(4):
                    pk = ptok[blk % 2]
                    r0 = t * 512 + blk * 128
                    P.dma('sp', lambda e, inc, pk=pk, r0=r0: inc(e.dma_start(out=pk[:], in_=p_d[sq_i, r0:r0 + 128, :])), 1, f'ptok{blk % 2}', writes=[f'ptok{blk % 2}'])
                    for c2 in range(2):
                        P.op('pe', lambda e, pk=pk, c2=c2: e.transpose(out=ps[7][:, c2 * 128:(c2 + 1) * 128], in_=pk[:, c2 * 128:(c2 + 1) * 128], identity=ident_f),
                             reads=[f'ptok{blk % 2}', 'cst'], writes=[PK[7]])
                    P.op('act', lambda e, blk=blk: e.activation(out=pT[:, :, blk * 128:(blk + 1) * 128], in_=ps[7][:, 0:256].rearrange("p (a b) -> p a b", a=2), func=AF.Copy),
                         reads=[PK[7]], writes=['pT'])
                for o in range(8):
                    bg, bp = o % 2, 2 + o % 2
                    for k in range(8):
                        P.op('pe', lambda e, k=k, o=o, bg=bg: e.matmul(ps[bg][:], lhsT=Wg[:, k, o * 128:(o + 1) * 128], rhs=xnp[:, k, tsl(t)], start=(k == 0), stop=(k == 7)),
                             reads=['Wg', xk(k, t)], writes=[PK[bg]])
                    for k2 in range(2):
                        P.op('pe', lambda e, k2=k2, o=o, bp=bp: e.matmul(ps[bp][:], lhsT=Wp[:, k2, o * 128:(o + 1) * 128], rhs=pT[:, k2, :], start=(k2 == 0), stop=(k2 == 1)),
                             reads=['Wp', 'pT'], writes=[PK[bp]])
                    P.op('act', lambda e, bg=bg: e.activation(out=sig[:], in_=ps[bg][:], func=AF.Sigmoid), reads=[PK[bg]], writes=['sig'])
                    P.op('dve', lambda e, bp=bp: e.tensor_tensor(out=tpl[:], in0=sig[:], in1=ps[bp][:], op=ALU.mult), reads=['sig', PK[bp]], writes=['tpl'])
                    P.op('dve', lambda e, o=o: e.tensor_tensor(out=hT[:, o, tsl(t)], in0=tpl[:], in1=hT[:, o, tsl(t)], op=ALU.add),
                         reads=['tpl', hk(o, t)], writes=[hk(o, t)])

        def fin_part(t):
            if stop_after is None:
                rms_norm_tile(t, G_FIN, yn, sq, lnv, 6, out_f32=True)
                src_of = lambda c, bs: yn[:, c, bs]
                rk = lambda c: [f'yn{c}']
            else:
                src_of = lambda c, bs: hT[:, c, slice(t * 512 + bs.start, t * 512 + bs.stop)]
                rk = lambda c: [hk(c, t)]
            for blk in range(4):
                bs = slice(blk * 128, (blk + 1) * 128)
                yt = ytok[ycnt_box[0] % 2]
                ytk = f'ytok{ycnt_box[0] % 2}'
                ycnt_box[0] += 1
                for half in range(2):
                    bank = 4 + half
                    for cc in range(4):
                        c = half * 4 + cc
                        sap = src_of(c, bs)
                        P.op('pe', lambda e, cc=cc, bank=bank, sap=sap: e.transpose(out=ps[bank][:, cc * 128:(cc + 1) * 128], in_=sap, identity=ident_f),
                             reads=rk(c) + ['cst'], writes=[PK[bank]])
                    if half == 0:
                        P.op('act', lambda e, yt=yt, bank=bank: e.activation(out=yt[:, 0:512], in_=ps[bank][:], func=AF.Copy), reads=[PK[bank]], writes=[ytk + 'a'])
                    else:
                        P.op('dve', lambda e, yt=yt, bank=bank: e.tensor_copy(out=yt[:, 512:1024], in_=ps[bank][:]), reads=[PK[bank]], writes=[ytk + 'b'])
                r0 = t * 512 + blk * 128
                P.dma('sp', lambda e, inc, yt=yt, r0=r0: inc(e.dma_start(out=y_d[sq_i, r0:r0 + 128, :], in_=yt[:])), 1, 'yout' + ytk,
                      reads=[ytk + 'a', ytk + 'b'], writes=[], out=True)
        ple_part(0)
        for t in range(NT):
            if t + 1 < NT:
                ple_part(t + 1)
            fin_part(t)
        P.barrier()

    for sq_i in range(nseq):
        run_seq(sq_i)
    P.emit()
    es.close()
    return nc


def _host_inputs(x, p, positions, ln_ffn1, w1_gate, w1_up, w1_down, ln_mix, w_in, hg_lb_logits, hg_out_norm,
                 q_a_norm, w_q_up, kv_a_norm, w_kv_up, w_out, ln_ffn2, w2_gate, w2_up, w2_down, ln_ple,
                 w_ple_gate, w_ple_proj, ln_final):
    f = lambda a: np.ascontiguousarray(np.asarray(a), dtype=np.float32)
    win = f(w_in)[0]
    kr0 = 2048 + 256 + 128
    winb = np.concatenate([win[:, 2048:2496], win[:, kr0 + 32:kr0 + 64], win[:, kr0:kr0 + 32]], axis=1)
    wq = f(w_q_up)[0]
    perm = []
    for h in range(4):
        b = h * 192 + 128
        perm += list(range(b + 32, b + 64)) + list(range(b, b + 32))
    wq_full = np.concatenate([wq, wq[:, perm]], axis=1)
    wkv = f(w_kv_up)[0].reshape(128, 4, 256)
    col = lambda v: np.asarray(v, np.float32).reshape(-1, 128).T
    gains = np.concatenate([col(ln_ffn1[0]), col(ln_mix[0]), col(ln_ffn2[0]), col(ln_ple[0]), col(ln_final),
                            col(q_a_norm[0]), col(kv_a_norm[0]), col(np.asarray(hg_out_norm)[0])], axis=1)
    assert gains.shape == (128, 47)
    idx = np.arange(128)
    bd = ((idx[:, None] // 64 == idx[None, :] // 64) & (idx[:, None] <= idx[None, :])).astype(np.float32)
    cm = (idx[None, :] >= idx[:, None]).astype(np.float32)
    ci = (idx[:, None] // 64 == np.arange(2)[None, :]).astype(np.float32)
    misc = np.zeros((128, 16), np.float32)
    misc[:, M_EPS] = EPS
    misc[:, M_ONE] = 1.0
    half = 32
    inv_freq = (np.float32(10000.0) ** (-np.arange(half, dtype=np.float32) / np.float32(half))).astype(np.float32)
    misc[0:64, M_INVF] = np.concatenate([inv_freq, inv_freq])
    misc[0:32, M_SIGN] = -1.0
    misc[32:64, M_SIGN] = 1.0
    cst = np.concatenate([np.eye(128, dtype=np.float32), bd, cm, np.ones((128, 128), np.float32), ci, misc], axis=1)
    assert cst.shape == (128, 530), cst.shape
    cst = np.ascontiguousarray(cst)
    shared = {
        "w1g": f(w1_gate)[0], "w1u": f(w1_up)[0], "w1d": f(w1_down)[0],
        "w2g": f(w2_gate)[0], "w2u": f(w2_up)[0], "w2d": f(w2_down)[0],
        "wina": np.ascontiguousarray(win[:, 0:2048]), "winb": np.ascontiguousarray(winb),
        "wq": np.ascontiguousarray(wq_full),
        "wkvk": np.ascontiguousarray(wkv[:, :, :128].reshape(128, 512)),
        "wkvv": np.ascontiguousarray(wkv[:, :, 128:].reshape(128, 512)),
        "wot": np.ascontiguousarray(f(w_out)[0][0:512]), "wob": np.ascontiguousarray(f(w_out)[0][512:1024]),
        "wpg": f(w_ple_gate)[0], "wpp": f(w_ple_proj)[0],
        "gains": np.ascontiguousarray(gains), "lbl": f(hg_lb_logits), "cst": cst,
    }
    xx = f(x)
    pp = f(p)[0]
    pos = np.ascontiguousarray(np.asarray(positions), dtype=np.int32)
    return shared, xx, pp, pos


def kernel(**inputs):
    shared, xx, pp, pos = _host_inputs(**inputs)
    ncores = 8
    stop = os.environ.get("MK_STOP") or None
    nc = build_program(stop_after=stop)
    in_maps = []
    for c in range(ncores):
        m = dict(shared)
        m["x"] = np.ascontiguousarray(xx[2 * c:2 * c + 2])
        m["p"] = np.ascontiguousarray(pp[2 * c:2 * c + 2])
        m["pos"] = np.ascontiguousarray(pos[2 * c:2 * c + 2])
        in_maps.append(m)
    res = run_bass_kernel_spmd(nc, in_maps, core_ids=list(range(ncores)))
    return np.concatenate([np.asarray(r["y"]) for r in res.results], axis=0).astype(np.float32)
```
